# Optimizing a Trainium2 kernel written in Bass

```python
import jax, jax.numpy as jnp
from jax import lax
import numpy as np

D_MODEL = 1024
BATCH = 8
SEQ = 2048
DEPTH = 2
DEC_BATCH = 128
DEC_SEQ = 8
PAST_LEN = 16384
PAGE_SIZE = 128

GLA_HEADS = 4
GLA_KEY_DIM = D_MODEL // 2
GLA_VAL_DIM = D_MODEL
GLA_HEAD_K = GLA_KEY_DIM // GLA_HEADS
GLA_HEAD_V = GLA_VAL_DIM // GLA_HEADS
GLA_GATE_RANK = 16
GLA_GATE_NORMALIZER = 16.0
GLA_CHUNK = 64
POOL_WIDTH = D_MODEL // 2
POOL_WINDOWS = (2, 4, 8, 16)
POOL_GROUPS = 4
POOL_GROUP_DIM = POOL_WIDTH // POOL_GROUPS
POOL_HIST = 15
SG_WIDTH = D_MODEL // 2
SG_GROUPS = 4
SG_GROUP_DIM = SG_WIDTH // SG_GROUPS
SG_CHUNK = 128
N_BRANCHES = 3
D_FF = 4 * D_MODEL
EPS = 1e-6
SPLIT_SIZES = (GLA_KEY_DIM, GLA_KEY_DIM, GLA_VAL_DIM, GLA_VAL_DIM, GLA_GATE_RANK,
               POOL_WIDTH, SG_WIDTH, SG_WIDTH, N_BRANCHES * D_MODEL)
IN_COLS = 2 * GLA_KEY_DIM + 2 * GLA_VAL_DIM + GLA_GATE_RANK + POOL_WIDTH + 2 * SG_WIDTH + N_BRANCHES * D_MODEL

kernel_name = "hybrid_gla_pool_sgmlp_decode_step"


def rms_norm(x, w):
    xf = x.astype(jnp.float32)
    y = xf * lax.rsqrt(jnp.mean(xf * xf, axis=-1, keepdims=True) + EPS)
    return (y * w.astype(jnp.float32)).astype(x.dtype)


def gla_mix(q, k, v, g_log, s0, chunk):
    n, t, _ = q.shape
    nc = t // chunk

    def heads(a, d):
        return a.astype(jnp.float32).reshape(n, nc, chunk, GLA_HEADS, d).transpose(1, 0, 3, 2, 4)

    qs = heads(q, GLA_HEAD_K) * (GLA_HEAD_K ** -0.5)
    ks = heads(k, GLA_HEAD_K)
    vs = heads(v, GLA_HEAD_V)
    gs = heads(g_log, GLA_HEAD_K)
    tri = jnp.tril(jnp.ones((chunk, chunk), dtype=bool))[:, :, None]

    def step(s, inp):
        qc, kc, vc, gc = inp
        b = jnp.cumsum(gc, axis=2)
        inter = jnp.einsum('nhtd,nhde->nhte', qc * jnp.exp(b), s)
        diff = b[:, :, :, None, :] - b[:, :, None, :, :]
        decay = jnp.exp(jnp.where(tri, diff, -jnp.inf))
        att = jnp.einsum('nhtd,nhsd,nhtsd->nhts', qc, kc, decay)
        o = inter + jnp.einsum('nhts,nhse->nhte', att, vc)
        b_last = b[:, :, -1:, :]
        s_new = jnp.exp(b_last[:, :, 0, :])[..., None] * s + jnp.einsum(
            'nhsd,nhse->nhde', kc * jnp.exp(b_last - b), vc)
        return s_new, o

    s_fin, o = lax.scan(step, s0.astype(jnp.float32), (qs, ks, vs, gs))
    o = o.transpose(1, 0, 3, 2, 4).reshape(n, t, GLA_HEADS, GLA_HEAD_V)
    return o, s_fin


def pool_mix(xp, hist, n_prev, w_mix, scale):
    n, t, _ = xp.shape
    xf = xp.astype(jnp.float32)
    full = jnp.concatenate([hist.astype(jnp.float32), xf], axis=1)
    cs = jnp.concatenate([jnp.zeros((n, 1, POOL_WIDTH), jnp.float32), jnp.cumsum(full, axis=1)], axis=1)
    end = cs[:, POOL_HIST + 1:]
    pos = jnp.arange(t)
    means = []
    for gi, w in enumerate(POOL_WINDOWS):
        lo, hi = gi * POOL_GROUP_DIM, (gi + 1) * POOL_GROUP_DIM
        start = cs[:, POOL_HIST + 1 - w: POOL_HIST + 1 - w + t, lo:hi]
        cnt = jnp.minimum(w, pos + 1 + n_prev).astype(jnp.float32)[None, :, None]
        means.append((end[:, :, lo:hi] - start) / cnt)
    pooled = (jnp.concatenate(means, axis=-1) - xf).reshape(n, t, POOL_GROUPS, POOL_GROUP_DIM)
    mixed = jnp.einsum('ntgc,gcd->ntgd', pooled, w_mix.astype(jnp.float32)).reshape(n, t, POOL_WIDTH)
    out = mixed * scale.astype(jnp.float32)
    return out.astype(xp.dtype), full[:, -POOL_HIST:]


def spatial_gate(u, v, w_s, b_s):
    n, t, _ = u.shape
    length = min(t, SG_CHUNK)
    nc = t // length
    w = w_s[:, :length, :length] * jnp.tril(jnp.ones((length, length), w_s.dtype))
    vg = v.reshape(n, nc, length, SG_GROUPS, SG_GROUP_DIM)
    mixed = jnp.einsum('gts,ncsgd->nctgd', w, vg) + b_s[:, :length].T[None, None, :, :, None]
    return u * mixed.reshape(n, t, SG_WIDTH)


def mixer_block(h, norm_mix, w_in, w_gk2, b_gk, gla_norm, w_pool_mix, pool_scale,
                w_spatial, b_spatial, w_br_a, w_br_b, w_br_c, w_out,
                s_gla, pool_hist, n_prev, gla_chunk):
    n, t, _ = h.shape
    hn = rms_norm(h, norm_mix)
    proj = hn @ w_in
    offs = [int(o) for o in np.cumsum(SPLIT_SIZES)[:-1]]
    q, k, v, g_out, g_lr, xp, u, vv, gates = jnp.split(proj, offs, axis=-1)
    g_log = jax.nn.log_sigmoid((g_lr @ w_gk2 + b_gk).astype(jnp.float32)) / GLA_GATE_NORMALIZER
    o, s_new = gla_mix(q, k, v, g_log, s_gla, gla_chunk)
    o = rms_norm(o, gla_norm) * jax.nn.silu(g_out.astype(jnp.float32).reshape(n, t, GLA_HEADS, GLA_HEAD_V))
    branch_a = o.reshape(n, t, GLA_VAL_DIM).astype(h.dtype) @ w_br_a
    pool_out, new_hist = pool_mix(xp, pool_hist, n_prev, w_pool_mix, pool_scale)
    branch_b = pool_out @ w_br_b
    u = jax.nn.gelu(u)
    vv = jax.nn.gelu(vv)
    branch_c = spatial_gate(u, vv, w_spatial, b_spatial) @ w_br_c
    ga, gb, gc = jnp.split(jax.nn.sigmoid(gates.astype(jnp.float32)), N_BRANCHES, axis=-1)
    merged = (ga * branch_a + gb * branch_b + gc * branch_c).astype(h.dtype)
    return h + merged @ w_out, s_new, new_hist, vv


def ffn_block(h, norm_ffn, w_ff1, w_ff2):
    hn = rms_norm(h, norm_ffn)
    a = jax.nn.relu(hn @ w_ff1)
    return h + (a * a) @ w_ff2


def setup_inputs(seed: int = 0) -> dict:
    key = jax.random.key(seed)
    ks = jax.random.split(key, 24)
    f32 = jnp.float32

    def nrm(k, shape, scale):
        return jax.random.normal(k, shape, f32) * scale

    return {
        "x_prompt": nrm(ks[0], (BATCH, SEQ, D_MODEL), 1.0),
        "x_sample": nrm(ks[1], (DEC_BATCH, DEC_SEQ, D_MODEL), 1.0),
        "state_gla": nrm(ks[2], (DEPTH, DEC_BATCH, GLA_HEADS, GLA_HEAD_K, GLA_HEAD_V), 1.0),
        "state_pool": nrm(ks[3], (DEPTH, DEC_BATCH, POOL_HIST, POOL_WIDTH), 1.0),
        "norm_mix": 1.0 + nrm(ks[4], (DEPTH, D_MODEL), 0.02),
        "w_in": nrm(ks[5], (DEPTH, D_MODEL, IN_COLS), D_MODEL ** -0.5),
        "w_gk2": nrm(ks[6], (DEPTH, GLA_GATE_RANK, GLA_KEY_DIM), GLA_GATE_RANK ** -0.5),
        "b_gk": nrm(ks[7], (DEPTH, GLA_KEY_DIM), 0.01),
        "gla_norm": 1.0 + nrm(ks[8], (DEPTH, GLA_HEAD_V), 0.02),
        "w_pool_mix": nrm(ks[9], (DEPTH, POOL_GROUPS, POOL_GROUP_DIM, POOL_GROUP_DIM), POOL_GROUP_DIM ** -0.5),
        "pool_scale": 1.0 + nrm(ks[10], (DEPTH, POOL_WIDTH), 0.02),
        "w_spatial": nrm(ks[11], (DEPTH, SG_GROUPS, SG_CHUNK, SG_CHUNK), SG_CHUNK ** -0.5),
        "b_spatial": 1.0 + nrm(ks[12], (DEPTH, SG_GROUPS, SG_CHUNK), 0.02),
        "w_br_a": nrm(ks[13], (DEPTH, GLA_VAL_DIM, D_MODEL), GLA_VAL_DIM ** -0.5),
        "w_br_b": nrm(ks[14], (DEPTH, POOL_WIDTH, D_MODEL), POOL_WIDTH ** -0.5),
        "w_br_c": nrm(ks[15], (DEPTH, SG_WIDTH, D_MODEL), SG_WIDTH ** -0.5),
        "w_out": nrm(ks[16], (DEPTH, D_MODEL, D_MODEL), D_MODEL ** -0.5),
        "norm_ffn": 1.0 + nrm(ks[17], (DEPTH, D_MODEL), 0.02),
        "w_ff1": nrm(ks[18], (DEPTH, D_MODEL, D_FF), D_MODEL ** -0.5),
        "w_ff2": nrm(ks[19], (DEPTH, D_FF, D_MODEL), D_FF ** -0.5),
        "norm_final": 1.0 + nrm(ks[20], (D_MODEL,), 0.02),
    }


def reference(x_prompt, x_sample, state_gla, state_pool, norm_mix, w_in, w_gk2, b_gk, gla_norm,
              w_pool_mix, pool_scale, w_spatial, b_spatial, w_br_a, w_br_b, w_br_c, w_out,
              norm_ffn, w_ff1, w_ff2, norm_final):
    hp, hs = x_prompt, x_sample
    gla_p, gla_s, pool_p, pool_s, sgv_s = [], [], [], [], []
    prompt_chunk = min(GLA_CHUNK, hp.shape[1])
    sample_chunk = hs.shape[1]
    sample_prev = min(PAST_LEN, POOL_HIST)
    for l in range(DEPTH):
        layer_w = (norm_mix[l], w_in[l], w_gk2[l], b_gk[l], gla_norm[l], w_pool_mix[l], pool_scale[l],
                   w_spatial[l], b_spatial[l], w_br_a[l], w_br_b[l], w_br_c[l], w_out[l])
        s0 = jnp.zeros((hp.shape[0], GLA_HEADS, GLA_HEAD_K, GLA_HEAD_V), jnp.float32)
        hist0 = jnp.zeros((hp.shape[0], POOL_HIST, POOL_WIDTH), jnp.float32)
        hp, sp, histp, _ = mixer_block(hp, *layer_w, s0, hist0, 0, prompt_chunk)
        hp = ffn_block(hp, norm_ffn[l], w_ff1[l], w_ff2[l])
        hs, ss, hists, vs = mixer_block(hs, *layer_w, state_gla[l], state_pool[l], sample_prev, sample_chunk)
        hs = ffn_block(hs, norm_ffn[l], w_ff1[l], w_ff2[l])
        gla_p.append(sp.astype(x_prompt.dtype))
        gla_s.append(ss.astype(state_gla.dtype))
        pool_p.append(histp.astype(x_prompt.dtype))
        pool_s.append(hists.astype(state_pool.dtype))
        sgv_s.append(vs)
    y_prompt = rms_norm(hp, norm_final)
    y_sample = rms_norm(hs, norm_final)
    return (y_prompt, y_sample, jnp.stack(gla_p), jnp.stack(gla_s), jnp.stack(pool_p),
            jnp.stack(pool_s), jnp.stack(sgv_s))
```

```python
import contextlib
import numpy as np
import concourse.bass as bass
import concourse.mybir as mybir
from concourse.bass_utils import run_bass_kernel_spmd

F32 = mybir.dt.float32
BF16 = mybir.dt.bfloat16
AF = mybir.ActivationFunctionType
ALU = mybir.AluOpType

NCORES = 8
D = 1024
KT = 8
SEQ = 2048
DEPTH = 2
NSEQ_S = 16
LS = 8
IN_COLS = 7696
EPS = 1e-6
C_Q, C_K, C_V, C_GO, C_GLR, C_XP, C_U, C_VV, C_GA, C_GB, C_GC = 0, 512, 1024, 2048, 3072, 3088, 3600, 4112, 4624, 5648, 6672
NSLOT = 5
TP = 512
WINDOWS = (2, 4, 8, 16)


class Eng:
    def __init__(self, nc, eng, name):
        self.e = eng
        self.name = name
        self.sem = nc.alloc_semaphore(name="sem_" + name)
        self.cnt = 0
        self.seen = {}

    def wait(self, tok):
        if tok is None:
            return
        sem, val = tok
        if sem is self.sem and self.name == "pe":
            return
        k = id(sem)
        if self.seen.get(k, 0) >= val:
            return
        self.e.wait_ge(sem, val)
        self.seen[k] = val

    def tok(self, inst):
        self.cnt += 1
        inst.then_inc(self.sem, 1)
        return (self.sem, self.cnt)


class B:
    def __init__(self, t):
        self.t = t
        self.w = None
        self.r = {}


def acquire(E, reads=(), writes=()):
    for b in reads:
        E.wait(b.w)
    for b in writes:
        E.wait(b.w)
        for t in list(b.r.values()):
            E.wait(t)


def release(tok, reads=(), writes=()):
    for b in reads:
        b.r[id(tok[0])] = tok
    for b in writes:
        b.w = tok
        b.r = {}


def build_program(do_prompt=True, do_sample=True, n_prompt_pass=4):
    nc = bass.Bass("TRN2", target_bir_lowering=False)

    def din(name, shape, dt=F32):
        return nc.dram_tensor(name, list(shape), dt, kind="ExternalInput").ap()

    def dout(name, shape):
        return nc.dram_tensor(name, list(shape), F32, kind="ExternalOutput").ap()

    x_p = din("xp", [SEQ, D])
    x_s = din("xs", [128, D])
    sgla = din("sgla", [DEPTH, NSEQ_S, 4, 128, 256])
    spool = din("spool", [DEPTH, NSEQ_S, 15, 512])
    w_in = din("w_in", [DEPTH, D, IN_COLS])
    w_gk2 = din("w_gk2", [DEPTH, 16, 512])
    b_gk = din("b_gk", [DEPTH, 512])
    w_pm = din("w_pool_mix", [DEPTH, 4, 128, 128])
    w_spT = din("w_spT", [DEPTH, 4, 128, 128])
    b_sp = din("b_spatial", [DEPTH, 4, 128])
    w_br_a = din("w_br_a", [DEPTH, D, D])
    w_br_b = din("w_br_b", [DEPTH, 512, D])
    w_br_c = din("w_br_c", [DEPTH, 512, D])
    w_out = din("w_out", [DEPTH, D, D])
    w_ff1 = din("w_ff1", [DEPTH, D, 4 * D])
    w_ff2 = din("w_ff2", [DEPTH, 4 * D, D])
    vecs = din("vecs", [128, 48])
    cmask = din("cmask", [128, 6, 128])
    ident_d = din("ident", [128, 128])
    r8_d = din("r8", [8, 128])
    mcol_d = din("mcol", [128, 16])
    invc_d = din("invc", [128, 4, 16])

    y_p = dout("yp", [SEQ, D])
    y_s = dout("ys", [128, D])
    gla_p = dout("gla_p", [DEPTH, 4, 128, 256])
    gla_s = dout("gla_s", [DEPTH, NSEQ_S, 4, 128, 256])
    pool_p = dout("pool_p", [DEPTH, 15, 512])
    pool_s = dout("pool_s", [DEPTH, NSEQ_S, 15, 512])
    sgv_s = dout("sgv_s", [DEPTH, 128, 512])

    NCHUNK = 37 * DEPTH
    wcache = nc.dram_tensor("wcache", [NCHUNK, 128, 4096], BF16, kind="Internal").ap()

    es = contextlib.ExitStack()
    with es:
        def sb(name, shape, dt=F32, stack=es):
            return B(stack.enter_context(nc.sbuf_tensor("sb_" + name, list(shape), dt)))

        PE = Eng(nc, nc.tensor, "pe")
        ACT = Eng(nc, nc.scalar, "act")
        DVE = Eng(nc, nc.vector, "dve")
        SP = Eng(nc, nc.sync, "sp")
        POOL = Eng(nc, nc.gpsimd, "pool")
        engines = [PE, ACT, DVE, SP, POOL]

        class DmaQ:
            def __init__(self, E, nsem, name):
                self.E = E
                self.sems = [nc.alloc_semaphore(name=f"dq_{name}{i}") for i in range(nsem)]
                self.cnt = [0] * nsem
                self.i = 0

            def dma(self, out, in_, reads=(), writes=(), **kw):
                k = self.i % len(self.sems)
                self.i += 1
                sem = self.sems[k]
                if self.cnt[k] > 0:
                    self.E.wait((sem, 16 * self.cnt[k]))
                acquire(self.E, reads, writes)
                self.E.e.dma_start(out=out, in_=in_, **kw).then_inc(sem, 16)
                self.cnt[k] += 1
                t = (sem, 16 * self.cnt[k])
                release(t, reads, writes)
                return t

            def all_toks(self):
                return [(s, 16 * c) for s, c in zip(self.sems, self.cnt) if c > 0]

        spq = DmaQ(SP, 16, "sp")
        plq = DmaQ(POOL, 8, "pl")

        def flat(b):
            return b.t[:, :, :].rearrange("p k t -> p (k t)")

        def op(E, fn, reads=(), writes=(), selfwait=False):
            acquire(E, reads, writes)
            if selfwait and E.cnt > 0:
                E.e.wait_ge(E.sem, E.cnt)
            inst = fn()
            t = E.tok(inst)
            release(t, reads, writes)
            return t

        def act(out, in_, func, reads, writes, bias=None, scale=None):
            kw = {}
            if bias is not None:
                kw["bias"] = bias
            if scale is not None:
                kw["scale"] = scale
            return op(ACT, lambda: nc.scalar.activation(out=out, in_=in_, func=func, **kw), reads, writes)

        def tt(out, in0, in1, alu, reads, writes, selfwait=False):
            return op(DVE, lambda: nc.vector.tensor_tensor(out=out, in0=in0, in1=in1, op=alu), reads, writes, selfwait)

        def stt(out, in0, scalar, in1, op0, op1, reads, writes):
            return op(DVE, lambda: nc.vector.scalar_tensor_tensor(out=out, in0=in0, scalar=scalar, in1=in1, op0=op0, op1=op1), reads, writes)

        def ts(out, in0, s1, op0, reads, writes, s2=None, op1=None):
            if op1 is None:
                return op(DVE, lambda: nc.vector.tensor_scalar(out=out, in0=in0, scalar1=s1, scalar2=None, op0=op0), reads, writes)
            return op(DVE, lambda: nc.vector.tensor_scalar(out=out, in0=in0, scalar1=s1, scalar2=s2, op0=op0, op1=op1), reads, writes)

        def pe_group(mms, reads, writes, late=None):
            acquire(PE, reads, writes)
            inst = None
            for i, m in enumerate(mms):
                kw = {}
                if len(m) > 5 and m[5]:
                    kw["skip_group_check"] = True
                if late is not None:
                    PE.wait(late[i].w)
                inst = nc.tensor.matmul(m[0], m[1], m[2], start=m[3], stop=m[4], **kw)
            t = PE.tok(inst)
            release(t, list(reads) + (list(late) if late is not None else []), writes)
            return t

        def pe_transposes(items, ident_ap, reads, writes):
            acquire(PE, reads, writes)
            inst = None
            for (o, i) in items:
                inst = nc.tensor.transpose(o, i, ident_ap)
            t = PE.tok(inst)
            release(t, reads, writes)
            return t

        PS = [B(es.enter_context(nc.psum_tensor(f"ps{i}", [128, 512], F32))) for i in range(8)]
        ps_i = [0]

        ps_hold = set()

        def psum(hold=False):
            while True:
                k = ps_i[0] % 8
                ps_i[0] += 1
                if k not in ps_hold:
                    break
            if hold:
                ps_hold.add(k)
            return PS[k]

        def ps_release(b):
            ps_hold.discard(PS.index(b))

        vec_t = sb("vec_t", [128, 48])
        cm = sb("cm", [128, 6, 128])
        ident = sb("ident", [128, 128])
        r8 = sb("r8", [8, 128])
        mcol = sb("mcolt", [128, 16])
        invc = sb("invct", [128, 4, 16])
        ones_bf = sb("ones_bf", [128, 128], BF16)
        wgk2_bf = sb("wgk2_bf", [128, DEPTH, 512], BF16)
        wpm_bf = sb("wpm_bf", [128, DEPTH, 4, 128], BF16)
        wsT_p = sb("wsT_p", [128, DEPTH, 4, 128], BF16)
        bsp_p = sb("bsp_p", [1, DEPTH, 4, 128], BF16)
        S_st = [[sb(f"S_{l}_{h}", [128, 256]) for h in range(4)] for l in range(DEPTH)]
        phist = [sb(f"phist{l}", [128, 4, 15]) for l in range(DEPTH)]
        ring_t = es.enter_context(nc.sbuf_tensor("wring", [128, NSLOT, 4096], BF16))
        slots = [B(ring_t) for _ in range(NSLOT)]
        slot_sem = [nc.alloc_semaphore(name=f"slot{i}") for i in range(NSLOT)]
        slot_cnt = [0] * NSLOT
        ring_i = [0]
        wglr = sb("wglr", [128, KT, 128], BF16)

        spq.dma(vec_t.t[:, :], vecs[:, :], writes=[vec_t])
        spq.dma(cm.t[:, :, :], cmask[:, :, :], writes=[cm])
        spq.dma(ident.t[:, :], ident_d[:, :], writes=[ident])
        spq.dma(r8.t[:, :], r8_d[:, :], writes=[r8])
        spq.dma(mcol.t[:, :], mcol_d[:, :], writes=[mcol])
        spq.dma(invc.t[:, :, :], invc_d[:, :, :], writes=[invc])
        wsp_b = slots[0]
        wsp_v = ring_t[:, 0, 0:2 * DEPTH * 4 * 128].bitcast(F32).rearrange("p (l g t) -> p l g t", l=DEPTH, g=4)
        spq.dma(wsp_v, w_spT.rearrange("l g s t -> s l g t"), writes=[wsp_b])
        op(DVE, lambda: nc.vector.memset(wgk2_bf.t[:, :, :], 0.0), writes=[wgk2_bf])
        op(DVE, lambda: nc.vector.memset(wglr.t[:, :, :], 0.0), writes=[wglr])
        plq.dma(wgk2_bf.t[0:16, :, :], w_gk2.rearrange("l r n -> r l n"), writes=[wgk2_bf])
        plq.dma(wgk2_bf.t[32:33, :, :], b_gk.rearrange("(o l) n -> o l n", o=1), writes=[wgk2_bf])
        plq.dma(wpm_bf.t[:, :, :, :], w_pm.rearrange("l g c d -> c l g d"), writes=[wpm_bf])
        plq.dma(bsp_p.t[:, :, :, :], b_sp.rearrange("(o l) g t -> o l g t", o=1), writes=[bsp_p])
        op(DVE, lambda: nc.vector.memset(ones_bf.t[:, :], 1.0), writes=[ones_bf])
        for l in range(DEPTH):
            for g in range(4):
                tt(wsT_p.t[:, l, g, :], wsp_v[:, l, g, :], cm.t[:, 0, :], ALU.mult, [wsp_b, cm], [wsT_p])
            for h in range(4):
                op(DVE, lambda l=l, h=h: nc.vector.memset(S_st[l][h].t[:, :], 0.0), writes=[S_st[l][h]])
            op(DVE, lambda l=l: nc.vector.memset(phist[l].t[:, :, :], 0.0), writes=[phist[l]])

        V_NMIX, V_NFFN, V_NFIN, V_GLAN, V_PSC = 0, 16, 32, 40, 44

        psc_d = din("psc", [128, 8])
        psc = sb("psc_t", [128, 8])
        spq.dma(psc.t[:, :], psc_d[:, :], writes=[psc])

        chunk_i = [0]
        cur_mode = [None]
        cache_tok = {}

        def wload(src_ap, kt, W):
            k = ring_i[0] % NSLOT
            ring_i[0] += 1
            n = chunk_i[0]
            chunk_i[0] += 1
            b = slots[k]
            view = ring_t[:, k, 0:kt * W].rearrange("p (k w) -> p k w", w=W)
            if n in cache_tok and cur_mode[0] == 's':
                acquire(SP, (), [b])
                SP.wait(cache_tok[n])
                nc.sync.dma_start(out=ring_t[:, k, 0:kt * W], in_=wcache[n][:, 0:kt * W]).then_inc(slot_sem[k], 16)
            elif n in cache_tok:
                acquire(POOL, (), [b])
                POOL.wait(cache_tok[n])
                nc.gpsimd.dma_start(out=ring_t[:, k, 0:kt * W], in_=wcache[n][:, 0:kt * W]).then_inc(slot_sem[k], 16)
            else:
                acquire(POOL, (), [b])
                nc.gpsimd.dma_start(out=view, in_=src_ap).then_inc(slot_sem[k], 16)
            slot_cnt[k] += 1
            b.w = (slot_sem[k], 16 * slot_cnt[k])
            b.r = {}
            if n not in cache_tok:
                cache_tok[n] = spq.dma(wcache[n][:, 0:kt * W], ring_t[:, k, 0:kt * W], reads=[b])
            return b, view

        def w_in_chunk(l, c0, W=512):
            return w_in[l].rearrange("(kt p) n -> p kt n", p=128)[:, :, c0:c0 + W]

        def barrier():
            toks = [(E.sem, E.cnt) for E in engines if E.cnt > 0] + spq.all_toks() + plq.all_toks()
            for E in engines:
                for t in toks:
                    E.wait(t)

        def run_pass(mode, pidx, T, tiles):
            G = T // 128
            c = tiles
            chunk_i[0] = 0
            cur_mode[0] = mode
            oq = plq if mode == 's' else spq
            nseq = 1 if mode == 'p' else NSEQ_S
            L = T // nseq
            mo = 0 if mode == 'p' else 3
            hT, hn, FA, FB, FC, H1, H2, H3, H4 = c["hT"], c["hn"], c["FA"], c["FB"], c["FC"], c["H1"], c["H2"], c["H3"], c["H4"]
            hnk = c["hnk"]
            x_src = x_p[pidx * TP:(pidx + 1) * TP, :] if mode == 'p' else x_s
            y_dst = y_p[pidx * TP:(pidx + 1) * TP, :] if mode == 'p' else y_s

            class NormAcc:
                def __init__(self, sqB):
                    self.ps = psum(hold=True)
                    self.sqB = sqB
                    self.pending = []

                def square(self, m):
                    act(self.sqB.t[:, m, :], hT.t[:, m, :], AF.Square, [hT], [self.sqB])
                    self.pending.append(m)

                def flush(self):
                    for m in self.pending:
                        pe_group([(self.ps.t[:, 0:T], ones_bf.t[:, :], self.sqB.t[:, m, :], m == 0, m == KT - 1)],
                                 [self.sqB, ones_bf], [self.ps])
                    self.pending = []

            def norm_finish(na, wcol0, dst_bs, dst_view, defer=False):
                na.flush()
                ps = na.ps
                act(c["rstd"].t[:, :], ps.t[:, 0:T], AF.Sqrt, [ps], [c["rstd"]], bias=EPS, scale=1.0 / D)
                ps_release(ps)
                op(DVE, lambda: nc.vector.reciprocal(out=c["rstd"].t[:, :], in_=c["rstd"].t[:, :]), [c["rstd"]], [c["rstd"]])
                if defer:
                    tt(c["rstd"].t[:, :], c["rstd"].t[:, :], c["rstd"].t[:, :], ALU.mult, [c["rstd"]], [c["rstd"]])
                    return
                for kt in range(KT):
                    stt(dst_view(kt), hT.t[:, kt, :], vec_t.t[:, wcol0 + kt:wcol0 + kt + 1], c["rstd"].t[:, :],
                        ALU.mult, ALU.mult, [hT, vec_t, c["rstd"]], [dst_bs[kt]])

            def proj_fm(slot, view, kt_n, col0, rhs_b, rhs_view):
                ps = psum()
                mms = [(ps.t[:, 0:T], view[:, kt, col0:col0 + 128], rhs_view(kt), kt == 0, kt == kt_n - 1) for kt in range(kt_n)]
                if rhs_b is hn:
                    pe_group(mms, [slot], [ps], late=hnk)
                else:
                    pe_group(mms, [slot, rhs_b], [ps])
                return ps

            def proj_tm(slot, view, j, W=512):
                ps = psum()
                pe_group([(ps.t[:, 0:W], hn.t[:, kt, j * 128:(j + 1) * 128], view[:, kt, 0:W], kt == 0, kt == KT - 1) for kt in range(KT)],
                         [slot], [ps], late=hnk)
                return ps

            hn_view = lambda kt: hn.t[:, kt, :]

            if mode == 's':
                def load_states(ll):
                    for q4 in range(4):
                        plq.dma(c["Sinb"].t[:, q4 * 4:(q4 + 1) * 4, :, :],
                                sgla[ll, q4 * 4:(q4 + 1) * 4].rearrange("i h d e -> d i h e"), writes=[c["SinbB"][q4]])

            xtok = flat(FA).rearrange("p (j f) -> p j f", f=D)
            if not c.get("x_prefetched"):
                oq.dma(xtok, x_src.rearrange("(j p) f -> p j f", p=128), writes=[FA])
            c["x_prefetched"] = False
            if mode == 's':
                for ll in range(DEPTH):
                    for hf in range(2):
                        oq.dma(c["hst"][ll].t[0:120, hf, :], spool[ll, hf * 8:(hf + 1) * 8].rearrange("i r c -> (i r) c"),
                                writes=[c["hst"][ll]])
                    oq.dma(c["w8"][ll].t[:, :, :], w_spT[ll][:, 0:8, 0:8].rearrange("g s t -> s g t"), writes=[c["w8"][ll]])
                    plq.dma(c["bsS"][ll].t[:, :, :, :],
                            b_sp[ll][:, 0:8].rearrange("(o g) t -> o g t", o=1).unsqueeze(2).broadcast_to([1, 4, NSEQ_S, LS]),
                            writes=[c["bsS"][ll]])
                    oq.dma(pool_s[ll][:, 0:7, :], spool[ll][:, 8:15, :])
            na = NormAcc(H1)
            for kt in range(KT):
                ps = psum()
                pe_transposes([(ps.t[:, j * 128:(j + 1) * 128], xtok[:, j, kt * 128:(kt + 1) * 128]) for j in range(G)],
                              ident.t[:, :], [FA, ident], [ps])
                na.flush()
                act(hT.t[:, kt, :], ps.t[:, 0:T], AF.Copy, [ps], [hT])
                na.square(kt)

            for l in range(DEPTH):
                norm_finish(na, V_NMIX + l * 8, hnk, lambda kt: hn.t[:, kt, :])

                glr, lsp, e1, e2 = c["glr"], FC, c["e1"], c["e2"]
                lv = flat(FC)[:, 0:G * 512].rearrange("p (j n) -> p j n", n=512)
                if not c.get("glr_init"):
                    op(DVE, lambda: nc.vector.memset(glr.t[:, :], 0.0), writes=[glr])
                    op(DVE, lambda: nc.vector.memset(glr.t[32:33, :], 1.0), writes=[glr], selfwait=True)
                    c["glr_init"] = True
                plq.dma(wglr.t[:, :, 0:16], w_in_chunk(l, C_GLR, 16), writes=[wglr])
                ps = psum()
                pe_group([(ps.t[:, 0:T], wglr.t[:, kt, :], hn.t[:, kt, :], kt == 0, kt == KT - 1) for kt in range(KT)],
                         [wglr], [ps], late=hnk)
                act(glr.t[0:32, :], ps.t[0:32, 0:T], AF.Copy, [ps], [glr])
                for j in range(G):
                    ps = psum()
                    pe_group([(ps.t[:, 0:512], glr.t[:, j * 128:(j + 1) * 128], wgk2_bf.t[:, l, :], True, True)],
                             [glr, wgk2_bf], [ps])
                    act(e1.t[:, :], ps.t[:, 0:512], AF.Exp, [ps], [e1], scale=-1.0)
                    act(lv[:, j, :], e1.t[:, :], AF.Ln, [e1], [lsp], bias=1.0)
                vtok = flat(H2).rearrange("p (j n) -> p j n", n=1024)
                for half in range(2):
                    slot, wv = wload(w_in_chunk(l, C_V + half * 512), KT, 512)
                    for j in range(G):
                        ps = proj_tm(slot, wv, j)
                        act(vtok[:, j, half * 512:(half + 1) * 512], ps.t[:, 0:512], AF.Copy, [ps], [H2])

                ebv = FB.t[:, 0:4, :]
                enbv = FB.t[:, 4:8, :]
                for hh in range(4):
                    ps = psum()
                    for j in range(G):
                        pe_group([(ps.t[:, j * 128:(j + 1) * 128], lv[:, j, hh * 128:(hh + 1) * 128], cm.t[:, mo + 1, :], True, True)],
                                 [lsp, cm], [ps])
                    act(ebv[:, hh, :], ps.t[:, 0:T], AF.Exp, [ps], [FB])
                    act(enbv[:, hh, :], ps.t[:, 0:T], AF.Exp, [ps], [FB], scale=-1.0)

                ktok = c["ktok"]
                kqv = H1.t[:, :, :].rearrange("p (a h) t -> p a h t", a=2)
                slot, wv = wload(w_in_chunk(l, C_K), KT, 512)
                for j in range(G):
                    psk = proj_tm(slot, wv, j)
                    psm = psum()
                    pe_group([(psm.t[:, 0:512], cm.t[:, mo + 2, :], lv[:, j, :], True, True)], [cm, lsp], [psm])
                    act(e2.t[:, :], psm.t[:, 0:512], AF.Exp, [psm], [e2])
                    tt(ktok.t[:, j, :], psk.t[:, 0:512], e2.t[:, :], ALU.mult, [psk, e2], [ktok])
                for hh in range(4):
                    ps = proj_fm(slot, wv, KT, hh * 128, hn, hn_view)
                    tt(kqv[:, 1, hh, :], ps.t[:, 0:T], enbv[:, hh, :], ALU.mult, [ps, FB], [H1])
                slot, wv = wload(w_in_chunk(l, C_Q), KT, 512)
                for hh in range(4):
                    ps = proj_fm(slot, wv, KT, hh * 128, hn, hn_view)
                    stt(kqv[:, 0, hh, :], ps.t[:, 0:T], 128.0 ** -0.5, ebv[:, hh, :], ALU.mult, ALU.mult, [ps, FB], [H1])
                for half in range(2):
                    slot, wv = wload(w_in_chunk(l, C_GO + half * 512), KT, 512)
                    for m4 in range(4):
                        ps = proj_fm(slot, wv, KT, m4 * 128, hn, hn_view)
                        act(H3.t[:, half * 4 + m4, :], ps.t[:, 0:T], AF.Silu, [ps], [H3])

                if mode == 's' and l == 0:
                    for ll in range(DEPTH):
                        w8, rep, wsS = c["w8"][ll], c["rep"][ll], c["wsS"][ll]
                        ps = psum()
                        pe_group([(ps.t[:, 0:32], r8.t[:, :], w8.t[:, :, :].rearrange("s g t -> s (g t)"), True, True)], [r8, w8], [ps])
                        act(rep.t[:, :], ps.t[:, 0:32], AF.Copy, [ps], [rep])
                        for g in range(4):
                            tt(wsS.t[:, g, :].rearrange("p (i t) -> p i t", t=LS),
                               cm.t[:, 3, :].rearrange("p (i t) -> p i t", t=LS),
                               rep.t[:, g * 8:(g + 1) * 8].unsqueeze(1).broadcast_to([128, NSEQ_S, LS]),
                               ALU.mult, [cm, rep], [wsS])
                oT = FA
                attT = c["attT"]
                Sb = c["Sb"]
                if mode == 'p':
                    for hh in range(4):
                        act(Sb[hh].t[:, :], S_st[l][hh].t[:, :], AF.Copy, [S_st[l][hh]], [Sb[hh]])
                else:
                    Sinb, SinbB, Sfp = c["Sinb"], c["SinbB"], c["Sfp"]
                    def load_sfp(i):
                        oq.dma(Sfp[i % 4].t[:, :, :], sgla[l, i].rearrange("h d e -> d h e"), writes=[Sfp[i % 4]])
                    load_sfp(0)
                    load_sfp(1)
                    load_sfp(2)
                    for i in range(NSEQ_S):
                        if i + 3 < NSEQ_S:
                            load_sfp(i + 3)
                        act(Sinb.t[:, i, :, :], Sfp[i % 4].t[:, :, :], AF.Copy, [Sfp[i % 4]], [SinbB[i // 4]])
                        km = c["kmask"][i % 2]
                        act(km.t[:, :], ktok.t[:, 0, :], AF.Copy, [ktok, mcol], [km], scale=mcol.t[:, i:i + 1])
                        so = c["Sout"][i % 4]
                        pss = [psum(), psum()]
                        for hh in range(4):
                            psx = pss[hh // 2]
                            pe_group([(psx.t[:, (hh % 2) * 256:(hh % 2 + 1) * 256], km.t[:, hh * 128:(hh + 1) * 128],
                                       vtok[:, 0, hh * 256:(hh + 1) * 256], True, True)], [km, H2], [psx])
                        for hh in range(4):
                            psx = pss[hh // 2]
                            stt(so.t[:, hh, :], Sfp[i % 4].t[:, hh, :], ebv[:, hh, i * LS + LS - 1:i * LS + LS],
                                psx.t[:, (hh % 2) * 256:(hh % 2 + 1) * 256], ALU.mult, ALU.add, [Sfp[i % 4], FB, psx], [so])
                        oq.dma(gla_s[l, i].rearrange("h d e -> d h e"), so.t[:, :, :], reads=[so])
                for j in range(G):
                    jsl = slice(j * 128, (j + 1) * 128)
                    for hh in range(4):
                        ps = psum()
                        pe_group([(ps.t[:, 0:128], kqv[:, 1, hh, jsl], kqv[:, 0, hh, jsl], True, True)], [H1], [ps])
                        tt(attT[hh].t[:, :], ps.t[:, 0:128], cm.t[:, mo, :], ALU.mult, [ps, cm], [attT[hh]])
                    for hh in range(4):
                        ps = psum()
                        for half in range(2):
                            osl = ps.t[:, half * 128:(half + 1) * 128]
                            vsl = vtok[:, j, hh * 256 + half * 128: hh * 256 + (half + 1) * 128]
                            if mode == 'p':
                                pe_group([(osl, vsl, attT[hh].t[:, :], True, False),
                                          (osl, Sb[hh].t[:, half * 128:(half + 1) * 128], kqv[:, 0, hh, jsl], False, True)],
                                         [H2, attT[hh], Sb[hh], H1], [ps])
                            else:
                                mms = [(osl, vsl, attT[hh].t[:, :], True, False, True)]
                                for i in range(NSEQ_S):
                                    mms.append((ps.t[:, half * 128 + i * LS: half * 128 + (i + 1) * LS],
                                                Sinb.t[:, i, hh, half * 128:(half + 1) * 128],
                                                kqv[:, 0, hh, i * LS:(i + 1) * LS], False, i == NSEQ_S - 1, True))
                                pe_group(mms, [H2, attT[hh], H1] + SinbB, [ps])
                        act(oT.t[:, 2 * hh:2 * hh + 2, jsl], ps.t[:, 0:256].rearrange("p (a t) -> p a t", a=2), AF.Copy, [ps], [oT])
                    if mode == 'p':
                        for hh in range(4):
                            ps = psum()
                            pe_group([(ps.t[:, 0:256], ktok.t[:, j, hh * 128:(hh + 1) * 128], vtok[:, j, hh * 256:(hh + 1) * 256], True, True)],
                                     [ktok, H2], [ps])
                            stt(S_st[l][hh].t[:, :], S_st[l][hh].t[:, :], ebv[:, hh, j * 128 + 127:j * 128 + 128], ps.t[:, 0:256],
                                ALU.mult, ALU.add, [S_st[l][hh], FB, ps], [S_st[l][hh]])
                            act(Sb[hh].t[:, :], S_st[l][hh].t[:, :], AF.Copy, [S_st[l][hh]], [Sb[hh]])
                if mode == 'p' and pidx == n_prompt_pass - 1:
                    for hh in range(4):
                        oq.dma(gla_p[l, hh], S_st[l][hh].t[:, :], reads=[S_st[l][hh]])

                sq = H1
                for hh in range(4):
                    act(sq.t[:, 2 * hh:2 * hh + 2, :], oT.t[:, 2 * hh:2 * hh + 2, :], AF.Square, [oT], [sq])
                xfull = c["xfull"]
                xf = xfull.t
                pooledv = H3.t[:, 0:4, :]
                if mode == 'p':
                    op(DVE, lambda: nc.vector.tensor_copy(out=xf[:, :, 0, 0:15], in_=phist[l].t[:, :, :]), [phist[l]], [xfull])
                else:
                    hs = c["hst"][l]
                    for g in range(4):
                        ps = psum()
                        pe_transposes([(ps.t[:, hf * 120:(hf + 1) * 120], hs.t[0:120, hf, g * 128:(g + 1) * 128]) for hf in range(2)],
                                      ident.t[0:120, 0:120], [hs, ident], [ps])
                        act(xf[:, g, :, 0:15], ps.t[:, 0:240].rearrange("p (i r) -> p i r", r=15), AF.Copy, [ps], [xfull])
                slot, wv = wload(w_in_chunk(l, C_XP), KT, 512)
                for g in range(4):
                    ps = proj_fm(slot, wv, KT, g * 128, hn, hn_view)
                    act(xf[:, g, :, 15:15 + L], ps.t[:, 0:T].rearrange("p (i t) -> p i t", t=L), AF.Copy, [ps], [xfull])
                if mode == 's':
                    ps = proj_tm(slot, wv, 0)
                    act(c["xptok"].t[:, :], ps.t[:, 0:512], AF.Copy, [ps], [c["xptok"]])
                    oq.dma(pool_s[l][:, 7:15, :], c["xptok"].t[:, :], reads=[c["xptok"]])
                elif pidx == n_prompt_pass - 1:
                    ps = proj_tm(slot, wv, G - 1)
                    act(c["xptok"].t[:, :], ps.t[:, 0:512], AF.Copy, [ps], [c["xptok"]])
                    oq.dma(pool_p[l], c["xptok"].t[113:128, :], reads=[c["xptok"]])
                for hh in range(4):
                    ps = psum()
                    pe_group([(ps.t[:, 0:T], ones_bf.t[:, :], sq.t[:, 2 * hh + a, :], a == 0, a == 1) for a in range(2)], [sq, ones_bf], [ps])
                    rso = c["rso"][hh % 2]
                    act(rso.t[:, :], ps.t[:, 0:T], AF.Sqrt, [ps], [rso], bias=EPS, scale=1.0 / 256)
                    op(DVE, lambda rso=rso: nc.vector.reciprocal(out=rso.t[:, :], in_=rso.t[:, :]), [rso], [rso])
                    for a in range(2):
                        tmp = c["tmp"][a]
                        stt(tmp.t[:, :], oT.t[:, 2 * hh + a, :], vec_t.t[:, V_GLAN + l * 2 + a:V_GLAN + l * 2 + a + 1], rso.t[:, :],
                            ALU.mult, ALU.mult, [oT, vec_t, rso], [tmp])
                        tt(H4.t[:, 2 * hh + a, :], tmp.t[:, :], H3.t[:, 2 * hh + a, :], ALU.mult, [tmp, H3], [H4])
                uT = c["uT"]
                vtkb = c["vtk"]
                vtk = vtkb.t
                slot, wv = wload(w_in_chunk(l, C_U), KT, 512)
                for g in range(4):
                    ps = proj_fm(slot, wv, KT, g * 128, hn, hn_view)
                    act(uT.t[:, g, :], ps.t[:, 0:T], AF.Gelu_apprx_tanh, [ps], [uT])
                slot, wv = wload(w_in_chunk(l, C_VV), KT, 512)
                for j in range(G):
                    ps = proj_tm(slot, wv, j)
                    if mode == 's':
                        act(c["xptok"].t[:, :], ps.t[:, 0:512], AF.Gelu_apprx_tanh, [ps], [c["xptok"]])
                        oq.dma(sgv_s[l], c["xptok"].t[:, :], reads=[c["xptok"]])
                        act(vtk[:, j, :], c["xptok"].t[:, :], AF.Copy, [c["xptok"]], [vtkb])
                    else:
                        act(vtk[:, j, :], ps.t[:, 0:512], AF.Gelu_apprx_tanh, [ps], [vtkb])
                pt = c["ptmp"]
                for g in range(4):
                    w = WINDOWS[g]
                    cur = xf[:, g]
                    lo = 0
                    k = 0
                    for st_ in (1, 2, 4, 8)[:g + 1]:
                        nxt = pt[k % 2].t
                        n_ = 15 + L - lo - st_
                        rd = [xfull] if k == 0 else [pt[(k - 1) % 2]]
                        tt(nxt[:, :, lo + st_:15 + L], cur[:, :, lo + st_:15 + L], cur[:, :, lo:lo + n_], ALU.add, rd, [pt[k % 2]])
                        cur = nxt
                        lo += st_
                        k += 1
                    wsb = pt[(k - 1) % 2]
                    pv = pooledv[:, g, :].rearrange("p (i t) -> p i t", t=L)
                    stt(pv, cur[:, :, 15:15 + L], 1.0 / w, xf[:, g, :, 15:15 + L], ALU.mult, ALU.subtract, [wsb, xfull], [H3])
                    if mode == 'p' and pidx == 0:
                        fx = c["fix"]
                        tt(fx.t[:, :], cur[:, 0, 15:31], invc.t[:, g, :], ALU.mult, [wsb, invc], [fx], selfwait=True)
                        tt(pooledv[:, g, 0:16], fx.t[:, :], xf[:, g, 0, 15:31], ALU.subtract, [fx, xfull], [H3], selfwait=True)
                if mode == 'p':
                    op(DVE, lambda: nc.vector.tensor_copy(out=phist[l].t[:, :, :], in_=xf[:, :, 0, T:T + 15]), [xfull], [phist[l]])
                OBv = H2.t[:, 0:4, :]
                OCv = H2.t[:, 4:8, :]
                if mode == 'p':
                    wsT_l = lambda g: wsT_p.t[:, l, g, :]
                    bsp_l = lambda g: bsp_p.t[0:1, l, g, :]
                    sg_reads = [wsT_p, bsp_p]
                else:
                    wsS, bsS = c["wsS"][l], c["bsS"][l]
                    wsT_l = lambda g: wsS.t[:, g, :]
                    bsp_l = lambda g: bsS.t[0:1, g, :, :].rearrange("o i t -> o (i t)")
                    sg_reads = [wsS, bsS]
                for j in range(G):
                    jsl = slice(j * 128, (j + 1) * 128)
                    ps = psum()
                    for g in range(4):
                        osl = ps.t[:, g * 128:(g + 1) * 128]
                        pe_group([(osl, vtk[:, j, g * 128:(g + 1) * 128], wsT_l(g), True, False),
                                  (osl, ones_bf.t[0:1, :], bsp_l(g), False, True)], [vtkb, ones_bf] + sg_reads, [ps])
                    tt(OCv[:, :, jsl], ps.t[:, 0:512].rearrange("p (g t) -> p g t", g=4), uT.t[:, :, jsl], ALU.mult, [ps, uT], [H2])

                acc = FA
                merged = H1
                sgb, tmpb = c["sg"], c["tmp"]
                cnt = 0
                for pi, (gc0, wsrc, ktn, srcb, srcv) in enumerate((
                        (C_GA, w_br_a, KT, H4, lambda kt: H4.t[:, kt, :]),
                        (C_GB, w_br_b, 4, H2, lambda kt: OBv[:, kt, :]),
                        (C_GC, w_br_c, 4, H2, lambda kt: OCv[:, kt, :]))):
                    bslot = bview = None
                    if pi == 1:
                        for g in range(4):
                            ps = psum()
                            pe_group([(ps.t[:, 0:T], wpm_bf.t[:, l, g, :], pooledv[:, g, :], True, True)], [wpm_bf, H3], [ps])
                            ts(OBv[:, g, :], ps.t[:, 0:T], psc.t[:, l * 4 + g:l * 4 + g + 1], ALU.mult, [ps, psc], [H2])
                    for half in range(2):
                        gslot, gview = wload(w_in_chunk(l, gc0 + half * 512), KT, 512)
                        if pi == 0:
                            bslot, bview = wload(wsrc[l].rearrange("(kt p) n -> p kt n", p=128)[:, :, half * 512:(half + 1) * 512], KT, 512)
                        elif half == 0:
                            bslot, bview = wload(wsrc[l].rearrange("(kt p) n -> p kt n", p=128), 4, 1024)
                        for m4 in range(4):
                            m = half * 4 + m4
                            psg = proj_fm(gslot, gview, KT, m4 * 128, hn, hn_view)
                            psb = proj_fm(bslot, bview, ktn, (m4 if pi == 0 else m) * 128, srcb, srcv)
                            sg = sgb[cnt % 2]
                            tmp = tmpb[cnt % 2]
                            cnt += 1
                            act(sg.t[:, :], psg.t[:, 0:T], AF.Sigmoid, [psg], [sg])
                            if pi == 0:
                                tt(acc.t[:, m, :], sg.t[:, :], psb.t[:, 0:T], ALU.mult, [sg, psb], [acc])
                            elif pi == 1:
                                tt(tmp.t[:, :], sg.t[:, :], psb.t[:, 0:T], ALU.mult, [sg, psb], [tmp])
                                tt(acc.t[:, m, :], acc.t[:, m, :], tmp.t[:, :], ALU.add, [acc, tmp], [acc])
                            else:
                                tt(tmp.t[:, :], sg.t[:, :], psb.t[:, 0:T], ALU.mult, [sg, psb], [tmp])
                                tt(merged.t[:, m, :], acc.t[:, m, :], tmp.t[:, :], ALU.add, [acc, tmp], [merged])
                na = NormAcc(H4)
                for half in range(2):
                    slot, wv = wload(w_out[l].rearrange("(kt p) n -> p kt n", p=128)[:, :, half * 512:(half + 1) * 512], KT, 512)
                    for m4 in range(4):
                        m = half * 4 + m4
                        ps = proj_fm(slot, wv, KT, m4 * 128, merged, lambda kt: merged.t[:, kt, :])
                        na.flush()
                        tt(hT.t[:, m, :], hT.t[:, m, :], ps.t[:, 0:T], ALU.add, [hT, ps], [hT])
                        na.square(m)
                        ts(hn.t[:, m, :], hT.t[:, m, :], vec_t.t[:, V_NFFN + l * 8 + m:V_NFFN + l * 8 + m + 1], ALU.mult,
                           [hT, vec_t], [hnk[m]])

                norm_finish(na, V_NFFN + l * 8, hnk, lambda kt: hn.t[:, kt, :], defer=True)
                a_b = [FB, FC]
                for ch in range(8):
                    slot, wv = wload(w_ff1[l].rearrange("(kt p) n -> p kt n", p=128)[:, :, ch * 512:(ch + 1) * 512], KT, 512)
                    for m4 in range(4):
                        m = ch * 4 + m4
                        ps = proj_fm(slot, wv, KT, m4 * 128, hn, hn_view)
                        tmp = c["tmp"][m % 2]
                        act(tmp.t[:, :], ps.t[:, 0:T], AF.Relu, [ps], [tmp])
                        tt(tmp.t[:, :], tmp.t[:, :], tmp.t[:, :], ALU.mult, [tmp], [tmp])
                        tt(c["a16"][m // 16][:, m % 16, :], tmp.t[:, :], c["rstd"].t[:, :], ALU.mult, [tmp, c["rstd"]], [a_b[m // 16]])
                na = NormAcc(H1)
                for cg in range(4):
                    pss = [psum(), psum()]
                    for kh in range(2):
                        src = w_ff2[l][kh * 2048:(kh + 1) * 2048, cg * 256:(cg + 1) * 256].rearrange("(kt p) n -> p kt n", p=128)
                        slot, wv = wload(src, 16, 256)
                        for mi in range(2):
                            pe_group([(pss[mi].t[:, 0:T], wv[:, kt, mi * 128:(mi + 1) * 128], c["a16"][kh][:, kt, :],
                                       kh == 0 and kt == 0, kh == 1 and kt == 15) for kt in range(16)],
                                     [slot, a_b[kh]], [pss[mi]])
                    na.flush()
                    for mi in range(2):
                        m = cg * 2 + mi
                        tt(hT.t[:, m, :], hT.t[:, m, :], pss[mi].t[:, 0:T], ALU.add, [hT, pss[mi]], [hT])
                        na.square(m)

            if mode == 'p' and pidx + 1 < n_prompt_pass:
                oq.dma(xtok, x_p[(pidx + 1) * TP:(pidx + 2) * TP, :].rearrange("(j p) f -> p j f", p=128), writes=[FA])
                c["x_prefetched"] = True
            yT = FB
            ykb = [B(FB.t) for _ in range(KT)]
            for b_ in ykb:
                b_.w, b_.r = FB.w, dict(FB.r)
            norm_finish(na, V_NFIN, ykb, lambda kt: FB.t[:, kt, :])
            ytok = flat(FC).rearrange("p (j f) -> p j f", f=D)
            for j in range(G):
                for half in range(2):
                    ps = psum()
                    pe_transposes([(ps.t[:, k4 * 128:(k4 + 1) * 128], yT.t[:, half * 4 + k4, j * 128:(j + 1) * 128]) for k4 in range(4)],
                                  ident.t[:, :], ykb[half * 4:half * 4 + 4] + [ident], [ps])
                    act(ytok[:, j, half * 512:(half + 1) * 512], ps.t[:, 0:512], AF.Copy, [ps], [FC])
            oq.dma(y_dst.rearrange("(j p) f -> p j f", p=128), ytok, reads=[FC])
            FB.w = ykb[KT - 1].w
            FB.r = {}
            for b_ in ykb:
                FB.r.update(b_.r)
            assert chunk_i[0] == NCHUNK, chunk_i[0]

        def alloc_tiles(stack, T, mode):
            G = T // 128
            nseq = 1 if mode == 'p' else NSEQ_S
            L = T // nseq
            sfx = mode
            c = {}
            c["hT"] = sb("hT" + sfx, [128, KT, T], F32, stack)
            c["hn"] = sb("hn" + sfx, [128, KT, T], BF16, stack)
            c["hnk"] = [B(c["hn"].t) for _ in range(KT)]
            for nm in ("FA", "FB", "FC"):
                c[nm] = sb(nm + sfx, [128, KT, T], F32, stack)
            for nm in ("H1", "H2", "H3", "H4"):
                c[nm] = sb(nm + sfx, [128, KT, T], BF16, stack)
            c["rstd"] = sb("rstd" + sfx, [128, T], F32, stack)
            c["glr"] = sb("glr" + sfx, [128, T], BF16, stack)
            c["ktok"] = sb("ktok" + sfx, [128, G, 512], BF16, stack)
            c["attT"] = [sb(f"attT{h}" + sfx, [128, 128], BF16, stack) for h in range(4)]
            c["Sb"] = [sb(f"Sb{h}" + sfx, [128, 256], BF16, stack) for h in range(4)]
            c["rso"] = [sb(f"rso{i}" + sfx, [128, T], F32, stack) for i in range(2)]
            c["tmp"] = [sb(f"tmp{i}" + sfx, [128, T], F32, stack) for i in range(2)]
            c["sg"] = c["rso"]
            if T == 512:
                c["e1"], c["e2"] = c["tmp"]
            else:
                c["e1"] = sb("e1" + sfx, [128, 512], F32, stack)
                c["e2"] = sb("e2" + sfx, [128, 512], F32, stack)
            c["xfull"] = sb("xfull" + sfx, [128, 4, nseq, 15 + L], F32, stack)
            c["ptmp"] = [sb(f"ptmp{i}" + sfx, [128, nseq, 15 + L], F32, stack) for i in range(2)]
            c["fix"] = sb("fix" + sfx, [128, 16], F32, stack)
            c["uT"] = sb("uT" + sfx, [128, 4, T], BF16, stack)
            c["vtk"] = sb("vtk" + sfx, [128, G, 512], BF16, stack)
            c["xptok"] = c["tmp"][0] if T == 512 else sb("xptok" + sfx, [128, 512], F32, stack)
            if mode == 's':
                c["Sinb"] = sb("Sinb", [128, NSEQ_S, 4, 256], BF16, stack)
                c["SinbB"] = [B(c["Sinb"].t) for _ in range(4)]
                c["Sfp"] = [sb(f"Sfp{i}", [128, 4, 256], F32, stack) for i in range(4)]
                c["Sout"] = [sb(f"Sout{i}", [128, 4, 256], F32, stack) for i in range(4)]
                c["kmask"] = [sb(f"kmask{i}", [128, 512], BF16, stack) for i in range(2)]
                c["hst"] = [sb(f"hst{i}", [128, 2, 512], F32, stack) for i in range(DEPTH)]
                c["w8"] = [sb(f"w8_{i}", [8, 4, 8], F32, stack) for i in range(DEPTH)]
                c["rep"] = [sb(f"rep{i}", [128, 32], F32, stack) for i in range(DEPTH)]
                c["wsS"] = [sb(f"wsS{i}", [128, 4, 128], BF16, stack) for i in range(DEPTH)]
                c["bsS"] = [sb(f"bsS{i}", [1, 4, NSEQ_S, 8], BF16, stack) for i in range(DEPTH)]
            c["a16"] = [flat(c[nm]).bitcast(BF16).rearrange("p (k t) -> p k t", t=T) for nm in ("FB", "FC")]
            return c

        if do_prompt:
            with contextlib.ExitStack() as st_p:
                c = alloc_tiles(st_p, TP, 'p')
                for p in range(n_prompt_pass):
                    run_pass('p', p, TP, c)
                barrier()
        if do_sample:
            with contextlib.ExitStack() as st_s:
                c = alloc_tiles(st_s, 128, 's')
                run_pass('s', 0, 128, c)
                barrier()
        for t in spq.all_toks() + plq.all_toks():
            SP.wait(t)
    return nc


_CACHE = {}


def _consts():
    s = np.arange(128)
    mask_p = (s[:, None] <= s[None, :]).astype(np.float32)
    same = (s[:, None] // LS == s[None, :] // LS)
    mask_s = (mask_p * same).astype(np.float32)
    m2_p = (s[:, None] > s[None, :]).astype(np.float32)
    m2_s = (m2_p * same).astype(np.float32)
    cm = np.stack([mask_p, -mask_p / 16.0, -m2_p / 16.0, mask_s, -mask_s / 16.0, -m2_s / 16.0], axis=1).astype(np.float32)
    ident = np.eye(128, dtype=np.float32)
    r8 = (s[None, :] % 8 == np.arange(8)[:, None]).astype(np.float32)
    mcol = (s[:, None] // LS == np.arange(16)[None, :]).astype(np.float32)
    invc = np.zeros((128, 4, 16), np.float32)
    for g, w in enumerate(WINDOWS):
        invc[:, g, :] = 1.0 / np.minimum(w, np.arange(16) + 1)
    return cm, ident, r8, mcol, invc


def kernel(x_prompt, x_sample, state_gla, state_pool, norm_mix, w_in, w_gk2, b_gk, gla_norm,
           w_pool_mix, pool_scale, w_spatial, b_spatial, w_br_a, w_br_b, w_br_c, w_out,
           norm_ffn, w_ff1, w_ff2, norm_final):
    f = lambda a: np.ascontiguousarray(np.asarray(a, dtype=np.float32))
    if "nc" not in _CACHE:
        _CACHE["nc"] = build_program()
    nc = _CACHE["nc"]
    vecs = np.zeros((128, 48), np.float32)
    nm, nf = f(norm_mix), f(norm_ffn)
    for l in range(DEPTH):
        vecs[:, l * 8:(l + 1) * 8] = nm[l].reshape(8, 128).T
        vecs[:, 16 + l * 8:16 + (l + 1) * 8] = nf[l].reshape(8, 128).T
        vecs[:, 40 + l * 2:40 + (l + 1) * 2] = f(gla_norm)[l].reshape(2, 128).T
    vecs[:, 32:40] = f(norm_final).reshape(8, 128).T
    psc = np.zeros((128, 8), np.float32)
    for l in range(DEPTH):
        psc[:, l * 4:(l + 1) * 4] = f(pool_scale)[l].reshape(4, 128).T
    cm, ident, r8, mcol, invc = _consts()
    shared = {
        "w_in": f(w_in), "w_gk2": f(w_gk2), "b_gk": f(b_gk), "w_pool_mix": f(w_pool_mix),
        "w_spT": np.ascontiguousarray(np.swapaxes(f(w_spatial), 2, 3)), "b_spatial": f(b_spatial),
        "w_br_a": f(w_br_a), "w_br_b": f(w_br_b), "w_br_c": f(w_br_c), "w_out": f(w_out),
        "w_ff1": f(w_ff1), "w_ff2": f(w_ff2), "vecs": vecs, "psc": psc, "cmask": cm, "ident": ident,
        "r8": r8, "mcol": mcol, "invc": invc,
    }
    xpr, xsm, sg, spl = f(x_prompt), f(x_sample), f(state_gla), f(state_pool)
    in_maps = []
    for cidx in range(NCORES):
        m = dict(shared)
        m["xp"] = xpr[cidx]
        m["xs"] = np.ascontiguousarray(xsm[cidx * NSEQ_S:(cidx + 1) * NSEQ_S].reshape(128, D))
        m["sgla"] = np.ascontiguousarray(sg[:, cidx * NSEQ_S:(cidx + 1) * NSEQ_S])
        m["spool"] = np.ascontiguousarray(spl[:, cidx * NSEQ_S:(cidx + 1) * NSEQ_S])
        in_maps.append(m)
    res = run_bass_kernel_spmd(nc, in_maps, core_ids=list(range(NCORES)))
    R = res.results
    y_prompt = np.stack([R[i]["yp"] for i in range(NCORES)], axis=0)
    y_sample = np.concatenate([R[i]["ys"].reshape(NSEQ_S, LS, D) for i in range(NCORES)], axis=0)
    gla_p = np.stack([R[i]["gla_p"] for i in range(NCORES)], axis=1)
    gla_s = np.concatenate([R[i]["gla_s"] for i in range(NCORES)], axis=1)
    pool_p = np.stack([R[i]["pool_p"] for i in range(NCORES)], axis=1)
    pool_s = np.concatenate([R[i]["pool_s"] for i in range(NCORES)], axis=1)
    sgv_s = np.concatenate([R[i]["sgv_s"].reshape(DEPTH, NSEQ_S, LS, 512) for i in range(NCORES)], axis=1)
    return (y_prompt.astype(np.float32), y_sample.astype(np.float32), gla_p.astype(np.float32), gla_s.astype(np.float32),
            pool_p.astype(np.float32), pool_s.astype(np.float32), sgv_s.astype(np.float32))
```

```python
import contextlib
import numpy as np
import concourse.bass as bass
import concourse.mybir as mybir
from concourse.bass_utils import run_bass_kernel_spmd

F32 = mybir.dt.float32
BF16 = mybir.dt.bfloat16
AF = mybir.ActivationFunctionType
ALU = mybir.AluOpType

NCORES = 8
D = 1024
KT = 8
SEQ = 2048
DEPTH = 2
NSEQ_S = 16
LS = 8
IN_COLS = 7696
EPS = 1e-6
C_Q, C_K, C_V, C_GO, C_GLR, C_XP, C_U, C_VV, C_GA, C_GB, C_GC = 0, 512, 1024, 2048, 3072, 3088, 3600, 4112, 4624, 5648, 6672
NSLOT = 5
TP = 512
WINDOWS = (2, 4, 8, 16)


class Eng:
    def __init__(self, nc, eng, name):
        self.e = eng
        self.name = name
        self.sem = nc.alloc_semaphore(name="sem_" + name)
        self.cnt = 0
        self.seen = {}

    def wait(self, tok):
        if tok is None:
            return
        sem, val = tok
        if sem is self.sem and self.name == "pe":
            return
        k = id(sem)
        if self.seen.get(k, 0) >= val:
            return
        self.e.wait_ge(sem, val)
        self.seen[k] = val

    def tok(self, inst):
        self.cnt += 1
        inst.then_inc(self.sem, 1)
        return (self.sem, self.cnt)


class B:
    def __init__(self, t):
        self.t = t
        self.w = None
        self.r = {}


def acquire(E, reads=(), writes=()):
    for b in reads:
        E.wait(b.w)
    for b in writes:
        E.wait(b.w)
        for t in list(b.r.values()):
            E.wait(t)


def release(tok, reads=(), writes=()):
    for b in reads:
        b.r[id(tok[0])] = tok
    for b in writes:
        b.w = tok
        b.r = {}


def build_program(do_prompt=True, do_sample=True, n_prompt_pass=4):
    nc = bass.Bass("TRN2", target_bir_lowering=False)

    def din(name, shape, dt=F32):
        return nc.dram_tensor(name, list(shape), dt, kind="ExternalInput").ap()

    def dout(name, shape):
        return nc.dram_tensor(name, list(shape), F32, kind="ExternalOutput").ap()

    x_p = din("xp", [SEQ, D])
    x_s = din("xs", [128, D])
    sgla = din("sgla", [DEPTH, NSEQ_S, 4, 128, 256])
    spool = din("spool", [DEPTH, NSEQ_S, 15, 512])
    w_in = din("w_in", [DEPTH, D, IN_COLS])
    w_gk2 = din("w_gk2", [DEPTH, 16, 512])
    b_gk = din("b_gk", [DEPTH, 512])
    w_pm = din("w_pool_mix", [DEPTH, 4, 128, 128])
    w_spT = din("w_spT", [DEPTH, 4, 128, 128])
    b_sp = din("b_spatial", [DEPTH, 4, 128])
    w_br_a = din("w_br_a", [DEPTH, D, D])
    w_br_b = din("w_br_b", [DEPTH, 512, D])
    w_br_c = din("w_br_c", [DEPTH, 512, D])
    w_out = din("w_out", [DEPTH, D, D])
    w_ff1 = din("w_ff1", [DEPTH, D, 4 * D])
    w_ff2 = din("w_ff2", [DEPTH, 4 * D, D])
    vecs = din("vecs", [128, 48])
    cmask = din("cmask", [128, 6, 128])
    ident_d = din("ident", [128, 128])
    r8_d = din("r8", [8, 128])
    mcol_d = din("mcol", [128, 16])
    invc_d = din("invc", [128, 4, 16])

    y_p = dout("yp", [SEQ, D])
    y_s = dout("ys", [128, D])
    gla_p = dout("gla_p", [DEPTH, 4, 128, 256])
    gla_s = dout("gla_s", [DEPTH, NSEQ_S, 4, 128, 256])
    pool_p = dout("pool_p", [DEPTH, 15, 512])
    pool_s = dout("pool_s", [DEPTH, NSEQ_S, 15, 512])
    sgv_s = dout("sgv_s", [DEPTH, 128, 512])

    NCHUNK = 37 * DEPTH
    wcache = nc.dram_tensor("wcache", [NCHUNK, 128, 4096], BF16, kind="Internal").ap()

    es = contextlib.ExitStack()
    with es:
        def sb(name, shape, dt=F32, stack=es):
            return B(stack.enter_context(nc.sbuf_tensor("sb_" + name, list(shape), dt)))

        PE = Eng(nc, nc.tensor, "pe")
        ACT = Eng(nc, nc.scalar, "act")
        DVE = Eng(nc, nc.vector, "dve")
        SP = Eng(nc, nc.sync, "sp")
        POOL = Eng(nc, nc.gpsimd, "pool")
        engines = [PE, ACT, DVE, SP, POOL]

        class DmaQ:
            def __init__(self, E, nsem, name):
                self.E = E
                self.sems = [nc.alloc_semaphore(name=f"dq_{name}{i}") for i in range(nsem)]
                self.cnt = [0] * nsem
                self.i = 0

            def dma(self, out, in_, reads=(), writes=(), **kw):
                k = self.i % len(self.sems)
                self.i += 1
                sem = self.sems[k]
                if self.cnt[k] > 0:
                    self.E.wait((sem, 16 * self.cnt[k]))
                acquire(self.E, reads, writes)
                self.E.e.dma_start(out=out, in_=in_, **kw).then_inc(sem, 16)
                self.cnt[k] += 1
                t = (sem, 16 * self.cnt[k])
                release(t, reads, writes)
                return t

            def all_toks(self):
                return [(s, 16 * c) for s, c in zip(self.sems, self.cnt) if c > 0]

        spq = DmaQ(SP, 16, "sp")
        plq = DmaQ(POOL, 8, "pl")

        def flat(b):
            return b.t[:, :, :].rearrange("p k t -> p (k t)")

        def op(E, fn, reads=(), writes=(), selfwait=False):
            acquire(E, reads, writes)
            if selfwait and E.cnt > 0:
                E.e.wait_ge(E.sem, E.cnt)
            inst = fn()
            t = E.tok(inst)
            release(t, reads, writes)
            return t

        def act(out, in_, func, reads, writes, bias=None, scale=None):
            kw = {}
            if bias is not None:
                kw["bias"] = bias
            if scale is not None:
                kw["scale"] = scale
            return op(ACT, lambda: nc.scalar.activation(out=out, in_=in_, func=func, **kw), reads, writes)

        def tt(out, in0, in1, alu, reads, writes, selfwait=False):
            return op(DVE, lambda: nc.vector.tensor_tensor(out=out, in0=in0, in1=in1, op=alu), reads, writes, selfwait)

        def stt(out, in0, scalar, in1, op0, op1, reads, writes):
            return op(DVE, lambda: nc.vector.scalar_tensor_tensor(out=out, in0=in0, scalar=scalar, in1=in1, op0=op0, op1=op1), reads, writes)

        def ts(out, in0, s1, op0, reads, writes, s2=None, op1=None):
            if op1 is None:
                return op(DVE, lambda: nc.vector.tensor_scalar(out=out, in0=in0, scalar1=s1, scalar2=None, op0=op0), reads, writes)
            return op(DVE, lambda: nc.vector.tensor_scalar(out=out, in0=in0, scalar1=s1, scalar2=s2, op0=op0, op1=op1), reads, writes)

        def pe_group(mms, reads, writes, late=None):
            acquire(PE, reads, writes)
            inst = None
            for i, m in enumerate(mms):
                kw = {}
                if len(m) > 5 and m[5]:
                    kw["skip_group_check"] = True
                if late is not None:
                    PE.wait(late[i].w)
                inst = nc.tensor.matmul(m[0], m[1], m[2], start=m[3], stop=m[4], **kw)
            t = PE.tok(inst)
            release(t, list(reads) + (list(late) if late is not None else []), writes)
            return t

        def pe_transposes(items, ident_ap, reads, writes):
            acquire(PE, reads, writes)
            inst = None
            for (o, i) in items:
                inst = nc.tensor.transpose(o, i, ident_ap)
            t = PE.tok(inst)
            release(t, reads, writes)
            return t

        PS = [B(es.enter_context(nc.psum_tensor(f"ps{i}", [128, 512], F32))) for i in range(8)]
        ps_i = [0]

        ps_hold = set()

        def psum(hold=False):
            while True:
                k = ps_i[0] % 8
                ps_i[0] += 1
                if k not in ps_hold:
                    break
            if hold:
                ps_hold.add(k)
            return PS[k]

        def ps_release(b):
            ps_hold.discard(PS.index(b))

        vec_t = sb("vec_t", [128, 48])
        cm = sb("cm", [128, 6, 128])
        ident = sb("ident", [128, 128])
        r8 = sb("r8", [8, 128])
        mcol = sb("mcolt", [128, 16])
        invc = sb("invct", [128, 4, 16])
        ones_bf = sb("ones_bf", [128, 128], BF16)
        wgk2_bf = sb("wgk2_bf", [128, DEPTH, 512], BF16)
        wpm_bf = sb("wpm_bf", [128, DEPTH, 4, 128], BF16)
        wsT_p = sb("wsT_p", [128, DEPTH, 4, 128], BF16)
        bsp_p = sb("bsp_p", [128, DEPTH, 4, 128], BF16)
        S_st = [[sb(f"S_{l}_{h}", [128, 256]) for h in range(4)] for l in range(DEPTH)]
        phist = [sb(f"phist{l}", [128, 4, 15]) for l in range(DEPTH)]
        ring_t = es.enter_context(nc.sbuf_tensor("wring", [128, NSLOT, 4096], BF16))
        slots = [B(ring_t) for _ in range(NSLOT)]
        slot_sem = [nc.alloc_semaphore(name=f"slot{i}") for i in range(NSLOT)]
        slot_cnt = [0] * NSLOT
        ring_i = [0]
        wglr = sb("wglr", [128, KT, 128], BF16)

        spq.dma(vec_t.t[:, :], vecs[:, :], writes=[vec_t])
        spq.dma(cm.t[:, :, :], cmask[:, :, :], writes=[cm])
        spq.dma(ident.t[:, :], ident_d[:, :], writes=[ident])
        spq.dma(r8.t[:, :], r8_d[:, :], writes=[r8])
        spq.dma(mcol.t[:, :], mcol_d[:, :], writes=[mcol])
        spq.dma(invc.t[:, :, :], invc_d[:, :, :], writes=[invc])
        wsp_b = slots[0]
        wsp_v = ring_t[:, 0, 0:2 * DEPTH * 4 * 128].bitcast(F32).rearrange("p (l g t) -> p l g t", l=DEPTH, g=4)
        spq.dma(wsp_v, w_spT.rearrange("l g s t -> s l g t"), writes=[wsp_b])
        op(DVE, lambda: nc.vector.memset(wgk2_bf.t[:, :, :], 0.0), writes=[wgk2_bf])
        op(DVE, lambda: nc.vector.memset(wglr.t[:, :, :], 0.0), writes=[wglr])
        plq.dma(wgk2_bf.t[0:16, :, :], w_gk2.rearrange("l r n -> r l n"), writes=[wgk2_bf])
        plq.dma(wgk2_bf.t[32:33, :, :], b_gk.rearrange("(o l) n -> o l n", o=1), writes=[wgk2_bf])
        plq.dma(wpm_bf.t[:, :, :, :], w_pm.rearrange("l g c d -> c l g d"), writes=[wpm_bf])
        op(DVE, lambda: nc.vector.memset(bsp_p.t[:, :, :, :], 0.0), writes=[bsp_p])
        plq.dma(bsp_p.t[0:1, :, :, :], b_sp.rearrange("(o l) g t -> o l g t", o=1), writes=[bsp_p])
        op(DVE, lambda: nc.vector.memset(ones_bf.t[:, :], 1.0), writes=[ones_bf])
        for l in range(DEPTH):
            for g in range(4):
                tt(wsT_p.t[:, l, g, :], wsp_v[:, l, g, :], cm.t[:, 0, :], ALU.mult, [wsp_b, cm], [wsT_p])
            for h in range(4):
                op(DVE, lambda l=l, h=h: nc.vector.memset(S_st[l][h].t[:, :], 0.0), writes=[S_st[l][h]])
            op(DVE, lambda l=l: nc.vector.memset(phist[l].t[:, :, :], 0.0), writes=[phist[l]])

        V_NMIX, V_NFFN, V_NFIN, V_GLAN, V_PSC = 0, 16, 32, 40, 44

        psc_d = din("psc", [128, 8])
        psc = sb("psc_t", [128, 8])
        spq.dma(psc.t[:, :], psc_d[:, :], writes=[psc])

        chunk_i = [0]
        cur_mode = [None]
        cache_tok = {}

        def wload(src_ap, kt, W):
            k = ring_i[0] % NSLOT
            ring_i[0] += 1
            n = chunk_i[0]
            chunk_i[0] += 1
            b = slots[k]
            view = ring_t[:, k, 0:kt * W].rearrange("p (k w) -> p k w", w=W)
            if n in cache_tok and cur_mode[0] == 's':
                acquire(SP, (), [b])
                SP.wait(cache_tok[n])
                nc.sync.dma_start(out=ring_t[:, k, 0:kt * W], in_=wcache[n][:, 0:kt * W]).then_inc(slot_sem[k], 16)
            elif n in cache_tok:
                acquire(POOL, (), [b])
                POOL.wait(cache_tok[n])
                nc.gpsimd.dma_start(out=ring_t[:, k, 0:kt * W], in_=wcache[n][:, 0:kt * W]).then_inc(slot_sem[k], 16)
            else:
                acquire(POOL, (), [b])
                nc.gpsimd.dma_start(out=view, in_=src_ap).then_inc(slot_sem[k], 16)
            slot_cnt[k] += 1
            b.w = (slot_sem[k], 16 * slot_cnt[k])
            b.r = {}
            if n not in cache_tok:
                cache_tok[n] = spq.dma(wcache[n][:, 0:kt * W], ring_t[:, k, 0:kt * W], reads=[b])
            return b, view

        def w_in_chunk(l, c0, W=512):
            return w_in[l].rearrange("(kt p) n -> p kt n", p=128)[:, :, c0:c0 + W]

        def barrier():
            toks = [(E.sem, E.cnt) for E in engines if E.cnt > 0] + spq.all_toks() + plq.all_toks()
            for E in engines:
                for t in toks:
                    E.wait(t)

        def run_pass(mode, pidx, T, tiles):
            G = T // 128
            c = tiles
            chunk_i[0] = 0
            cur_mode[0] = mode
            oq = plq if mode == 's' else spq
            nseq = 1 if mode == 'p' else NSEQ_S
            L = T // nseq
            mo = 0 if mode == 'p' else 3
            hT, hn, FA, FB, FC, H1, H2, H3, H4 = c["hT"], c["hn"], c["FA"], c["FB"], c["FC"], c["H1"], c["H2"], c["H3"], c["H4"]
            hnk = c["hnk"]
            x_src = x_p[pidx * TP:(pidx + 1) * TP, :] if mode == 'p' else x_s
            y_dst = y_p[pidx * TP:(pidx + 1) * TP, :] if mode == 'p' else y_s

            class NormAcc:
                def __init__(self, sqB):
                    self.ps = psum(hold=True)
                    self.sqB = sqB
                    self.pending = []

                def square(self, m):
                    act(self.sqB.t[:, m, :], hT.t[:, m, :], AF.Square, [hT], [self.sqB])
                    self.pending.append(m)

                def flush(self):
                    for m in self.pending:
                        pe_group([(self.ps.t[:, 0:T], ones_bf.t[:, :], self.sqB.t[:, m, :], m == 0, m == KT - 1)],
                                 [self.sqB, ones_bf], [self.ps])
                    self.pending = []

            def norm_finish(na, wcol0, dst_bs, dst_view, defer=False):
                na.flush()
                ps = na.ps
                act(c["rstd"].t[:, :], ps.t[:, 0:T], AF.Sqrt, [ps], [c["rstd"]], bias=EPS, scale=1.0 / D)
                ps_release(ps)
                op(DVE, lambda: nc.vector.reciprocal(out=c["rstd"].t[:, :], in_=c["rstd"].t[:, :]), [c["rstd"]], [c["rstd"]])
                if defer:
                    tt(c["rstd"].t[:, :], c["rstd"].t[:, :], c["rstd"].t[:, :], ALU.mult, [c["rstd"]], [c["rstd"]])
                    return
                for kt in range(KT):
                    stt(dst_view(kt), hT.t[:, kt, :], vec_t.t[:, wcol0 + kt:wcol0 + kt + 1], c["rstd"].t[:, :],
                        ALU.mult, ALU.mult, [hT, vec_t, c["rstd"]], [dst_bs[kt]])

            def proj_fm(slot, view, kt_n, col0, rhs_b, rhs_view):
                ps = psum()
                mms = [(ps.t[:, 0:T], view[:, kt, col0:col0 + 128], rhs_view(kt), kt == 0, kt == kt_n - 1) for kt in range(kt_n)]
                if rhs_b is hn:
                    pe_group(mms, [slot], [ps], late=hnk)
                else:
                    pe_group(mms, [slot, rhs_b], [ps])
                return ps

            def proj_tm(slot, view, j, W=512):
                ps = psum()
                pe_group([(ps.t[:, 0:W], hn.t[:, kt, j * 128:(j + 1) * 128], view[:, kt, 0:W], kt == 0, kt == KT - 1) for kt in range(KT)],
                         [slot], [ps], late=hnk)
                return ps

            hn_view = lambda kt: hn.t[:, kt, :]

            if mode == 's':
                def load_states(ll):
                    for q4 in range(4):
                        plq.dma(c["Sinb"].t[:, q4 * 4:(q4 + 1) * 4, :, :],
                                sgla[ll, q4 * 4:(q4 + 1) * 4].rearrange("i h d e -> d i h e"), writes=[c["SinbB"][q4]])

            xtok = flat(FA).rearrange("p (j f) -> p j f", f=D)
            if not c.get("x_prefetched"):
                oq.dma(xtok, x_src.rearrange("(j p) f -> p j f", p=128), writes=[FA])
            c["x_prefetched"] = False
            if mode == 's':
                for ll in range(DEPTH):
                    for hf in range(2):
                        oq.dma(c["hst"][ll].t[0:120, hf, :], spool[ll, hf * 8:(hf + 1) * 8].rearrange("i r c -> (i r) c"),
                                writes=[c["hst"][ll]])
                    oq.dma(c["w8"][ll].t[:, :, :], w_spT[ll][:, 0:8, 0:8].rearrange("g s t -> s g t"), writes=[c["w8"][ll]])
                    op(DVE, lambda ll=ll: nc.vector.memset(c["bsS"][ll].t[:, :, :, :], 0.0), writes=[c["bsS"][ll]])
                    plq.dma(c["bsS"][ll].t[0:1, :, :, :],
                            b_sp[ll][:, 0:8].rearrange("(o g) t -> o g t", o=1).unsqueeze(2).broadcast_to([1, 4, NSEQ_S, LS]),
                            writes=[c["bsS"][ll]])
                    oq.dma(pool_s[ll][:, 0:7, :], spool[ll][:, 8:15, :])
            na = NormAcc(H1)
            for kt in range(KT):
                ps = psum()
                pe_transposes([(ps.t[:, j * 128:(j + 1) * 128], xtok[:, j, kt * 128:(kt + 1) * 128]) for j in range(G)],
                              ident.t[:, :], [FA, ident], [ps])
                na.flush()
                act(hT.t[:, kt, :], ps.t[:, 0:T], AF.Copy, [ps], [hT])
                na.square(kt)

            for l in range(DEPTH):
                norm_finish(na, V_NMIX + l * 8, hnk, lambda kt: hn.t[:, kt, :])

                glr, lsp, e1, e2 = c["glr"], FC, c["e1"], c["e2"]
                lv = flat(FC)[:, 0:G * 512].rearrange("p (j n) -> p j n", n=512)
                if not c.get("glr_init"):
                    op(DVE, lambda: nc.vector.memset(glr.t[:, :], 0.0), writes=[glr])
                    op(DVE, lambda: nc.vector.memset(glr.t[32:33, :], 1.0), writes=[glr], selfwait=True)
                    c["glr_init"] = True
                plq.dma(wglr.t[:, :, 0:16], w_in_chunk(l, C_GLR, 16), writes=[wglr])
                ps = psum()
                pe_group([(ps.t[:, 0:T], wglr.t[:, kt, :], hn.t[:, kt, :], kt == 0, kt == KT - 1) for kt in range(KT)],
                         [wglr], [ps], late=hnk)
                act(glr.t[0:32, :], ps.t[0:32, 0:T], AF.Copy, [ps], [glr])
                for j in range(G):
                    ps = psum()
                    pe_group([(ps.t[:, 0:512], glr.t[:, j * 128:(j + 1) * 128], wgk2_bf.t[:, l, :], True, True)],
                             [glr, wgk2_bf], [ps])
                    act(e1.t[:, :], ps.t[:, 0:512], AF.Exp, [ps], [e1], scale=-1.0)
                    act(lv[:, j, :], e1.t[:, :], AF.Ln, [e1], [lsp], bias=1.0)
                vtok = flat(H2).rearrange("p (j n) -> p j n", n=1024)
                for half in range(2):
                    slot, wv = wload(w_in_chunk(l, C_V + half * 512), KT, 512)
                    for j in range(G):
                        ps = proj_tm(slot, wv, j)
                        act(vtok[:, j, half * 512:(half + 1) * 512], ps.t[:, 0:512], AF.Copy, [ps], [H2])

                ebv = FB.t[:, 0:4, :]
                enbv = FB.t[:, 4:8, :]
                for hh in range(4):
                    ps = psum()
                    for j in range(G):
                        pe_group([(ps.t[:, j * 128:(j + 1) * 128], lv[:, j, hh * 128:(hh + 1) * 128], cm.t[:, mo + 1, :], True, True)],
                                 [lsp, cm], [ps])
                    act(ebv[:, hh, :], ps.t[:, 0:T], AF.Exp, [ps], [FB])
                    act(enbv[:, hh, :], ps.t[:, 0:T], AF.Exp, [ps], [FB], scale=-1.0)

                ktok = c["ktok"]
                kqv = H1.t[:, :, :].rearrange("p (a h) t -> p a h t", a=2)
                slot, wv = wload(w_in_chunk(l, C_K), KT, 512)
                for j in range(G):
                    psk = proj_tm(slot, wv, j)
                    psm = psum()
                    pe_group([(psm.t[:, 0:512], cm.t[:, mo + 2, :], lv[:, j, :], True, True)], [cm, lsp], [psm])
                    act(e2.t[:, :], psm.t[:, 0:512], AF.Exp, [psm], [e2])
                    tt(ktok.t[:, j, :], psk.t[:, 0:512], e2.t[:, :], ALU.mult, [psk, e2], [ktok])
                for hh in range(4):
                    ps = proj_fm(slot, wv, KT, hh * 128, hn, hn_view)
                    tt(kqv[:, 1, hh, :], ps.t[:, 0:T], enbv[:, hh, :], ALU.mult, [ps, FB], [H1])
                slot, wv = wload(w_in_chunk(l, C_Q), KT, 512)
                for hh in range(4):
                    ps = proj_fm(slot, wv, KT, hh * 128, hn, hn_view)
                    stt(kqv[:, 0, hh, :], ps.t[:, 0:T], 128.0 ** -0.5, ebv[:, hh, :], ALU.mult, ALU.mult, [ps, FB], [H1])
                for half in range(2):
                    slot, wv = wload(w_in_chunk(l, C_GO + half * 512), KT, 512)
                    for m4 in range(4):
                        ps = proj_fm(slot, wv, KT, m4 * 128, hn, hn_view)
                        act(H3.t[:, half * 4 + m4, :], ps.t[:, 0:T], AF.Silu, [ps], [H3])

                if mode == 's' and l == 0:
                    for ll in range(DEPTH):
                        w8, rep, wsS = c["w8"][ll], c["rep"][ll], c["wsS"][ll]
                        ps = psum()
                        pe_group([(ps.t[:, 0:32], r8.t[:, :], w8.t[:, :, :].rearrange("s g t -> s (g t)"), True, True)], [r8, w8], [ps])
                        act(rep.t[:, :], ps.t[:, 0:32], AF.Copy, [ps], [rep])
                        for g in range(4):
                            tt(wsS.t[:, g, :].rearrange("p (i t) -> p i t", t=LS),
                               cm.t[:, 3, :].rearrange("p (i t) -> p i t", t=LS),
                               rep.t[:, g * 8:(g + 1) * 8].unsqueeze(1).broadcast_to([128, NSEQ_S, LS]),
                               ALU.mult, [cm, rep], [wsS])
                oT = FA
                attT = c["attT"]
                Sb = c["Sb"]
                if mode == 'p':
                    for hh in range(4):
                        act(Sb[hh].t[:, :], S_st[l][hh].t[:, :], AF.Copy, [S_st[l][hh]], [Sb[hh]])
                else:
                    Sinb, SinbB, Sfp = c["Sinb"], c["SinbB"], c["Sfp"]
                    def load_sfp(i):
                        oq.dma(Sfp[i % 4].t[:, :, :], sgla[l, i].rearrange("h d e -> d h e"), writes=[Sfp[i % 4]])
                    load_sfp(0)
                    load_sfp(1)
                    load_sfp(2)
                    for i in range(NSEQ_S):
                        if i + 3 < NSEQ_S:
                            load_sfp(i + 3)
                        act(Sinb.t[:, i, :, :], Sfp[i % 4].t[:, :, :], AF.Copy, [Sfp[i % 4]], [SinbB[i // 4]])
                        km = c["kmask"][i % 2]
                        act(km.t[:, :], ktok.t[:, 0, :], AF.Copy, [ktok, mcol], [km], scale=mcol.t[:, i:i + 1])
                        so = c["Sout"][i % 4]
                        pss = [psum(), psum()]
                        for hh in range(4):
                            psx = pss[hh // 2]
                            pe_group([(psx.t[:, (hh % 2) * 256:(hh % 2 + 1) * 256], km.t[:, hh * 128:(hh + 1) * 128],
                                       vtok[:, 0, hh * 256:(hh + 1) * 256], True, True)], [km, H2], [psx])
                        for hh in range(4):
                            psx = pss[hh // 2]
                            stt(so.t[:, hh, :], Sfp[i % 4].t[:, hh, :], ebv[:, hh, i * LS + LS - 1:i * LS + LS],
                                psx.t[:, (hh % 2) * 256:(hh % 2 + 1) * 256], ALU.mult, ALU.add, [Sfp[i % 4], FB, psx], [so])
                        oq.dma(gla_s[l, i].rearrange("h d e -> d h e"), so.t[:, :, :], reads=[so])
                for j in range(G):
                    jsl = slice(j * 128, (j + 1) * 128)
                    for hh in range(4):
                        ps = psum()
                        pe_group([(ps.t[:, 0:128], kqv[:, 1, hh, jsl], kqv[:, 0, hh, jsl], True, True)], [H1], [ps])
                        tt(attT[hh].t[:, :], ps.t[:, 0:128], cm.t[:, mo, :], ALU.mult, [ps, cm], [attT[hh]])
                    for hh in range(4):
                        ps = psum()
                        for half in range(2):
                            osl = ps.t[:, half * 128:(half + 1) * 128]
                            vsl = vtok[:, j, hh * 256 + half * 128: hh * 256 + (half + 1) * 128]
                            if mode == 'p':
                                pe_group([(osl, vsl, attT[hh].t[:, :], True, False),
                                          (osl, Sb[hh].t[:, half * 128:(half + 1) * 128], kqv[:, 0, hh, jsl], False, True)],
                                         [H2, attT[hh], Sb[hh], H1], [ps])
                            else:
                                mms = [(osl, vsl, attT[hh].t[:, :], True, False, True)]
                                for i in range(NSEQ_S):
                                    mms.append((ps.t[:, half * 128 + i * LS: half * 128 + (i + 1) * LS],
                                                Sinb.t[:, i, hh, half * 128:(half + 1) * 128],
                                                kqv[:, 0, hh, i * LS:(i + 1) * LS], False, i == NSEQ_S - 1, True))
                                pe_group(mms, [H2, attT[hh], H1] + SinbB, [ps])
                        act(oT.t[:, 2 * hh:2 * hh + 2, jsl], ps.t[:, 0:256].rearrange("p (a t) -> p a t", a=2), AF.Copy, [ps], [oT])
                    if mode == 'p':
                        for hh in range(4):
                            ps = psum()
                            pe_group([(ps.t[:, 0:256], ktok.t[:, j, hh * 128:(hh + 1) * 128], vtok[:, j, hh * 256:(hh + 1) * 256], True, True)],
                                     [ktok, H2], [ps])
                            stt(S_st[l][hh].t[:, :], S_st[l][hh].t[:, :], ebv[:, hh, j * 128 + 127:j * 128 + 128], ps.t[:, 0:256],
                                ALU.mult, ALU.add, [S_st[l][hh], FB, ps], [S_st[l][hh]])
                            act(Sb[hh].t[:, :], S_st[l][hh].t[:, :], AF.Copy, [S_st[l][hh]], [Sb[hh]])
                if mode == 'p' and pidx == n_prompt_pass - 1:
                    for hh in range(4):
                        oq.dma(gla_p[l, hh], S_st[l][hh].t[:, :], reads=[S_st[l][hh]])

                sq = H1
                for hh in range(4):
                    act(sq.t[:, 2 * hh:2 * hh + 2, :], oT.t[:, 2 * hh:2 * hh + 2, :], AF.Square, [oT], [sq])
                xfull = c["xfull"]
                xf = xfull.t
                pooledv = H3.t[:, 0:4, :]
                if mode == 'p':
                    op(DVE, lambda: nc.vector.tensor_copy(out=xf[:, :, 0, 0:15], in_=phist[l].t[:, :, :]), [phist[l]], [xfull])
                else:
                    hs = c["hst"][l]
                    for g in range(4):
                        ps = psum()
                        pe_transposes([(ps.t[:, hf * 120:(hf + 1) * 120], hs.t[0:120, hf, g * 128:(g + 1) * 128]) for hf in range(2)],
                                      ident.t[0:120, 0:120], [hs, ident], [ps])
                        act(xf[:, g, :, 0:15], ps.t[:, 0:240].rearrange("p (i r) -> p i r", r=15), AF.Copy, [ps], [xfull])
                slot, wv = wload(w_in_chunk(l, C_XP), KT, 512)
                for g in range(4):
                    ps = proj_fm(slot, wv, KT, g * 128, hn, hn_view)
                    act(xf[:, g, :, 15:15 + L], ps.t[:, 0:T].rearrange("p (i t) -> p i t", t=L), AF.Copy, [ps], [xfull])
                if mode == 's':
                    ps = proj_tm(slot, wv, 0)
                    act(c["xptok"].t[:, :], ps.t[:, 0:512], AF.Copy, [ps], [c["xptok"]])
                    oq.dma(pool_s[l][:, 7:15, :], c["xptok"].t[:, :], reads=[c["xptok"]])
                elif pidx == n_prompt_pass - 1:
                    ps = proj_tm(slot, wv, G - 1)
                    act(c["xptok"].t[:, :], ps.t[:, 0:512], AF.Copy, [ps], [c["xptok"]])
                    oq.dma(pool_p[l], c["xptok"].t[113:128, :], reads=[c["xptok"]])
                for hh in range(4):
                    ps = psum()
                    pe_group([(ps.t[:, 0:T], ones_bf.t[:, :], sq.t[:, 2 * hh + a, :], a == 0, a == 1) for a in range(2)], [sq, ones_bf], [ps])
                    rso = c["rso"][hh % 2]
                    act(rso.t[:, :], ps.t[:, 0:T], AF.Sqrt, [ps], [rso], bias=EPS, scale=1.0 / 256)
                    op(DVE, lambda rso=rso: nc.vector.reciprocal(out=rso.t[:, :], in_=rso.t[:, :]), [rso], [rso])
                    for a in range(2):
                        tmp = c["tmp"][a]
                        stt(tmp.t[:, :], oT.t[:, 2 * hh + a, :], vec_t.t[:, V_GLAN + l * 2 + a:V_GLAN + l * 2 + a + 1], rso.t[:, :],
                            ALU.mult, ALU.mult, [oT, vec_t, rso], [tmp])
                        tt(H4.t[:, 2 * hh + a, :], tmp.t[:, :], H3.t[:, 2 * hh + a, :], ALU.mult, [tmp, H3], [H4])
                uT = c["uT"]
                vtkb = c["vtk"]
                vtk = vtkb.t
                slot, wv = wload(w_in_chunk(l, C_U), KT, 512)
                for g in range(4):
                    ps = proj_fm(slot, wv, KT, g * 128, hn, hn_view)
                    act(uT.t[:, g, :], ps.t[:, 0:T], AF.Gelu_apprx_tanh, [ps], [uT])
                slot, wv = wload(w_in_chunk(l, C_VV), KT, 512)
                for j in range(G):
                    ps = proj_tm(slot, wv, j)
                    if mode == 's':
                        act(c["xptok"].t[:, :], ps.t[:, 0:512], AF.Gelu_apprx_tanh, [ps], [c["xptok"]])
                        oq.dma(sgv_s[l], c["xptok"].t[:, :], reads=[c["xptok"]])
                        act(vtk[:, j, :], c["xptok"].t[:, :], AF.Copy, [c["xptok"]], [vtkb])
                    else:
                        act(vtk[:, j, :], ps.t[:, 0:512], AF.Gelu_apprx_tanh, [ps], [vtkb])
                pt = c["ptmp"]
                for g in range(4):
                    w = WINDOWS[g]
                    cur = xf[:, g]
                    lo = 0
                    k = 0
                    for st_ in (1, 2, 4, 8)[:g + 1]:
                        nxt = pt[k % 2].t
                        n_ = 15 + L - lo - st_
                        rd = [xfull] if k == 0 else [pt[(k - 1) % 2]]
                        tt(nxt[:, :, lo + st_:15 + L], cur[:, :, lo + st_:15 + L], cur[:, :, lo:lo + n_], ALU.add, rd, [pt[k % 2]])
                        cur = nxt
                        lo += st_
                        k += 1
                    wsb = pt[(k - 1) % 2]
                    pv = pooledv[:, g, :].rearrange("p (i t) -> p i t", t=L)
                    stt(pv, cur[:, :, 15:15 + L], 1.0 / w, xf[:, g, :, 15:15 + L], ALU.mult, ALU.subtract, [wsb, xfull], [H3])
                    if mode == 'p' and pidx == 0:
                        fx = c["fix"]
                        tt(fx.t[:, :], cur[:, 0, 15:31], invc.t[:, g, :], ALU.mult, [wsb, invc], [fx], selfwait=True)
                        tt(pooledv[:, g, 0:16], fx.t[:, :], xf[:, g, 0, 15:31], ALU.subtract, [fx, xfull], [H3], selfwait=True)
                if mode == 'p':
                    op(DVE, lambda: nc.vector.tensor_copy(out=phist[l].t[:, :, :], in_=xf[:, :, 0, T:T + 15]), [xfull], [phist[l]])
                OBv = H2.t[:, 0:4, :]
                OCv = H2.t[:, 4:8, :]
                if mode == 'p':
                    wsT_l = lambda g: wsT_p.t[:, l, g, :]
                    bsp_l = lambda g: bsp_p.t[:, l, g, :]
                    sg_reads = [wsT_p, bsp_p]
                else:
                    wsS, bsS = c["wsS"][l], c["bsS"][l]
                    wsT_l = lambda g: wsS.t[:, g, :]
                    bsp_l = lambda g: bsS.t[:, g, :, :].rearrange("o i t -> o (i t)")
                    sg_reads = [wsS, bsS]
                for j in range(G):
                    jsl = slice(j * 128, (j + 1) * 128)
                    ps = psum()
                    for g in range(4):
                        osl = ps.t[:, g * 128:(g + 1) * 128]
                        pe_group([(osl, vtk[:, j, g * 128:(g + 1) * 128], wsT_l(g), True, False),
                                  (osl, ones_bf.t[:, :], bsp_l(g), False, True)], [vtkb, ones_bf] + sg_reads, [ps])
                    tt(OCv[:, :, jsl], ps.t[:, 0:512].rearrange("p (g t) -> p g t", g=4), uT.t[:, :, jsl], ALU.mult, [ps, uT], [H2])

                acc = FA
                merged = H1
                sgb, tmpb = c["sg"], c["tmp"]
                cnt = 0
                for pi, (gc0, wsrc, ktn, srcb, srcv) in enumerate((
                        (C_GA, w_br_a, KT, H4, lambda kt: H4.t[:, kt, :]),
                        (C_GB, w_br_b, 4, H2, lambda kt: OBv[:, kt, :]),
                        (C_GC, w_br_c, 4, H2, lambda kt: OCv[:, kt, :]))):
                    bslot = bview = None
                    if pi == 1:
                        for g in range(4):
                            ps = psum()
                            pe_group([(ps.t[:, 0:T], wpm_bf.t[:, l, g, :], pooledv[:, g, :], True, True)], [wpm_bf, H3], [ps])
                            ts(OBv[:, g, :], ps.t[:, 0:T], psc.t[:, l * 4 + g:l * 4 + g + 1], ALU.mult, [ps, psc], [H2])
                    for half in range(2):
                        gslot, gview = wload(w_in_chunk(l, gc0 + half * 512), KT, 512)
                        if pi == 0:
                            bslot, bview = wload(wsrc[l].rearrange("(kt p) n -> p kt n", p=128)[:, :, half * 512:(half + 1) * 512], KT, 512)
                        elif half == 0:
                            bslot, bview = wload(wsrc[l].rearrange("(kt p) n -> p kt n", p=128), 4, 1024)
                        for m4 in range(4):
                            m = half * 4 + m4
                            psg = proj_fm(gslot, gview, KT, m4 * 128, hn, hn_view)
                            psb = proj_fm(bslot, bview, ktn, (m4 if pi == 0 else m) * 128, srcb, srcv)
                            sg = sgb[cnt % 2]
                            tmp = tmpb[cnt % 2]
                            cnt += 1
                            act(sg.t[:, :], psg.t[:, 0:T], AF.Sigmoid, [psg], [sg])
                            if pi == 0:
                                tt(acc.t[:, m, :], sg.t[:, :], psb.t[:, 0:T], ALU.mult, [sg, psb], [acc])
                            elif pi == 1:
                                tt(tmp.t[:, :], sg.t[:, :], psb.t[:, 0:T], ALU.mult, [sg, psb], [tmp])
                                tt(acc.t[:, m, :], acc.t[:, m, :], tmp.t[:, :], ALU.add, [acc, tmp], [acc])
                            else:
                                tt(tmp.t[:, :], sg.t[:, :], psb.t[:, 0:T], ALU.mult, [sg, psb], [tmp])
                                tt(merged.t[:, m, :], acc.t[:, m, :], tmp.t[:, :], ALU.add, [acc, tmp], [merged])
                na = NormAcc(H4)
                for half in range(2):
                    slot, wv = wload(w_out[l].rearrange("(kt p) n -> p kt n", p=128)[:, :, half * 512:(half + 1) * 512], KT, 512)
                    for m4 in range(4):
                        m = half * 4 + m4
                        ps = proj_fm(slot, wv, KT, m4 * 128, merged, lambda kt: merged.t[:, kt, :])
                        na.flush()
                        tt(hT.t[:, m, :], hT.t[:, m, :], ps.t[:, 0:T], ALU.add, [hT, ps], [hT])
                        na.square(m)
                        ts(hn.t[:, m, :], hT.t[:, m, :], vec_t.t[:, V_NFFN + l * 8 + m:V_NFFN + l * 8 + m + 1], ALU.mult,
                           [hT, vec_t], [hnk[m]])

                norm_finish(na, V_NFFN + l * 8, hnk, lambda kt: hn.t[:, kt, :], defer=True)
                a_b = [FB, FC]
                for ch in range(8):
                    slot, wv = wload(w_ff1[l].rearrange("(kt p) n -> p kt n", p=128)[:, :, ch * 512:(ch + 1) * 512], KT, 512)
                    for m4 in range(4):
                        m = ch * 4 + m4
                        ps = proj_fm(slot, wv, KT, m4 * 128, hn, hn_view)
                        tmp = c["tmp"][m % 2]
                        act(tmp.t[:, :], ps.t[:, 0:T], AF.Relu, [ps], [tmp])
                        tt(tmp.t[:, :], tmp.t[:, :], tmp.t[:, :], ALU.mult, [tmp], [tmp])
                        tt(c["a16"][m // 16][:, m % 16, :], tmp.t[:, :], c["rstd"].t[:, :], ALU.mult, [tmp, c["rstd"]], [a_b[m // 16]])
                na = NormAcc(H1)
                for cg in range(4):
                    pss = [psum(), psum()]
                    for kh in range(2):
                        src = w_ff2[l][kh * 2048:(kh + 1) * 2048, cg * 256:(cg + 1) * 256].rearrange("(kt p) n -> p kt n", p=128)
                        slot, wv = wload(src, 16, 256)
                        for mi in range(2):
                            pe_group([(pss[mi].t[:, 0:T], wv[:, kt, mi * 128:(mi + 1) * 128], c["a16"][kh][:, kt, :],
                                       kh == 0 and kt == 0, kh == 1 and kt == 15) for kt in range(16)],
                                     [slot, a_b[kh]], [pss[mi]])
                    na.flush()
                    for mi in range(2):
                        m = cg * 2 + mi
                        tt(hT.t[:, m, :], hT.t[:, m, :], pss[mi].t[:, 0:T], ALU.add, [hT, pss[mi]], [hT])
                        na.square(m)

            if mode == 'p' and pidx + 1 < n_prompt_pass:
                oq.dma(xtok, x_p[(pidx + 1) * TP:(pidx + 2) * TP, :].rearrange("(j p) f -> p j f", p=128), writes=[FA])
                c["x_prefetched"] = True
            yT = FB
            ykb = [B(FB.t) for _ in range(KT)]
            for b_ in ykb:
                b_.w, b_.r = FB.w, dict(FB.r)
            norm_finish(na, V_NFIN, ykb, lambda kt: FB.t[:, kt, :])
            ytok = flat(FC).rearrange("p (j f) -> p j f", f=D)
            for j in range(G):
                for half in range(2):
                    ps = psum()
                    pe_transposes([(ps.t[:, k4 * 128:(k4 + 1) * 128], yT.t[:, half * 4 + k4, j * 128:(j + 1) * 128]) for k4 in range(4)],
                                  ident.t[:, :], ykb[half * 4:half * 4 + 4] + [ident], [ps])
                    act(ytok[:, j, half * 512:(half + 1) * 512], ps.t[:, 0:512], AF.Copy, [ps], [FC])
            oq.dma(y_dst.rearrange("(j p) f -> p j f", p=128), ytok, reads=[FC])
            FB.w = ykb[KT - 1].w
            FB.r = {}
            for b_ in ykb:
                FB.r.update(b_.r)
            assert chunk_i[0] == NCHUNK, chunk_i[0]

        def alloc_tiles(stack, T, mode):
            G = T // 128
            nseq = 1 if mode == 'p' else NSEQ_S
            L = T // nseq
            sfx = mode
            c = {}
            c["hT"] = sb("hT" + sfx, [128, KT, T], F32, stack)
            c["hn"] = sb("hn" + sfx, [128, KT, T], BF16, stack)
            c["hnk"] = [B(c["hn"].t) for _ in range(KT)]
            for nm in ("FA", "FB", "FC"):
                c[nm] = sb(nm + sfx, [128, KT, T], F32, stack)
            for nm in ("H1", "H2", "H3", "H4"):
                c[nm] = sb(nm + sfx, [128, KT, T], BF16, stack)
            c["rstd"] = sb("rstd" + sfx, [128, T], F32, stack)
            c["glr"] = sb("glr" + sfx, [128, T], BF16, stack)
            c["ktok"] = sb("ktok" + sfx, [128, G, 512], BF16, stack)
            c["attT"] = [sb(f"attT{h}" + sfx, [128, 128], BF16, stack) for h in range(4)]
            c["Sb"] = [sb(f"Sb{h}" + sfx, [128, 256], BF16, stack) for h in range(4)]
            c["rso"] = [sb(f"rso{i}" + sfx, [128, T], F32, stack) for i in range(2)]
            c["tmp"] = [sb(f"tmp{i}" + sfx, [128, T], F32, stack) for i in range(2)]
            c["sg"] = c["rso"]
            if T == 512:
                c["e1"], c["e2"] = c["tmp"]
            else:
                c["e1"] = sb("e1" + sfx, [128, 512], F32, stack)
                c["e2"] = sb("e2" + sfx, [128, 512], F32, stack)
            c["xfull"] = sb("xfull" + sfx, [128, 4, nseq, 15 + L], F32, stack)
            c["ptmp"] = [sb(f"ptmp{i}" + sfx, [128, nseq, 15 + L], F32, stack) for i in range(2)]
            c["fix"] = sb("fix" + sfx, [128, 16], F32, stack)
            c["uT"] = sb("uT" + sfx, [128, 4, T], BF16, stack)
            c["vtk"] = sb("vtk" + sfx, [128, G, 512], BF16, stack)
            c["xptok"] = c["tmp"][0] if T == 512 else sb("xptok" + sfx, [128, 512], F32, stack)
            if mode == 's':
                c["Sinb"] = sb("Sinb", [128, NSEQ_S, 4, 256], BF16, stack)
                c["SinbB"] = [B(c["Sinb"].t) for _ in range(4)]
                c["Sfp"] = [sb(f"Sfp{i}", [128, 4, 256], F32, stack) for i in range(4)]
                c["Sout"] = [sb(f"Sout{i}", [128, 4, 256], F32, stack) for i in range(4)]
                c["kmask"] = [sb(f"kmask{i}", [128, 512], BF16, stack) for i in range(2)]
                c["hst"] = [sb(f"hst{i}", [128, 2, 512], F32, stack) for i in range(DEPTH)]
                c["w8"] = [sb(f"w8_{i}", [8, 4, 8], F32, stack) for i in range(DEPTH)]
                c["rep"] = [sb(f"rep{i}", [128, 32], F32, stack) for i in range(DEPTH)]
                c["wsS"] = [sb(f"wsS{i}", [128, 4, 128], BF16, stack) for i in range(DEPTH)]
                c["bsS"] = [sb(f"bsS{i}", [128, 4, NSEQ_S, 8], BF16, stack) for i in range(DEPTH)]
            c["a16"] = [flat(c[nm]).bitcast(BF16).rearrange("p (k t) -> p k t", t=T) for nm in ("FB", "FC")]
            return c

        if do_prompt:
            with contextlib.ExitStack() as st_p:
                c = alloc_tiles(st_p, TP, 'p')
                for p in range(n_prompt_pass):
                    run_pass('p', p, TP, c)
                barrier()
        if do_sample:
            with contextlib.ExitStack() as st_s:
                c = alloc_tiles(st_s, 128, 's')
                run_pass('s', 0, 128, c)
                barrier()
        for t in spq.all_toks() + plq.all_toks():
            SP.wait(t)
    return nc


_CACHE = {}


def _consts():
    s = np.arange(128)
    mask_p = (s[:, None] <= s[None, :]).astype(np.float32)
    same = (s[:, None] // LS == s[None, :] // LS)
    mask_s = (mask_p * same).astype(np.float32)
    m2_p = (s[:, None] > s[None, :]).astype(np.float32)
    m2_s = (m2_p * same).astype(np.float32)
    cm = np.stack([mask_p, -mask_p / 16.0, -m2_p / 16.0, mask_s, -mask_s / 16.0, -m2_s / 16.0], axis=1).astype(np.float32)
    ident = np.eye(128, dtype=np.float32)
    r8 = (s[None, :] % 8 == np.arange(8)[:, None]).astype(np.float32)
    mcol = (s[:, None] // LS == np.arange(16)[None, :]).astype(np.float32)
    invc = np.zeros((128, 4, 16), np.float32)
    for g, w in enumerate(WINDOWS):
        invc[:, g, :] = 1.0 / np.minimum(w, np.arange(16) + 1)
    return cm, ident, r8, mcol, invc


def kernel(x_prompt, x_sample, state_gla, state_pool, norm_mix, w_in, w_gk2, b_gk, gla_norm,
           w_pool_mix, pool_scale, w_spatial, b_spatial, w_br_a, w_br_b, w_br_c, w_out,
           norm_ffn, w_ff1, w_ff2, norm_final):
    f = lambda a: np.ascontiguousarray(np.asarray(a, dtype=np.float32))
    if "nc" not in _CACHE:
        _CACHE["nc"] = build_program()
    nc = _CACHE["nc"]
    vecs = np.zeros((128, 48), np.float32)
    nm, nf = f(norm_mix), f(norm_ffn)
    for l in range(DEPTH):
        vecs[:, l * 8:(l + 1) * 8] = nm[l].reshape(8, 128).T
        vecs[:, 16 + l * 8:16 + (l + 1) * 8] = nf[l].reshape(8, 128).T
        vecs[:, 40 + l * 2:40 + (l + 1) * 2] = f(gla_norm)[l].reshape(2, 128).T
    vecs[:, 32:40] = f(norm_final).reshape(8, 128).T
    psc = np.zeros((128, 8), np.float32)
    for l in range(DEPTH):
        psc[:, l * 4:(l + 1) * 4] = f(pool_scale)[l].reshape(4, 128).T
    cm, ident, r8, mcol, invc = _consts()
    shared = {
        "w_in": f(w_in), "w_gk2": f(w_gk2), "b_gk": f(b_gk), "w_pool_mix": f(w_pool_mix),
        "w_spT": np.ascontiguousarray(np.swapaxes(f(w_spatial), 2, 3)), "b_spatial": f(b_spatial),
        "w_br_a": f(w_br_a), "w_br_b": f(w_br_b), "w_br_c": f(w_br_c), "w_out": f(w_out),
        "w_ff1": f(w_ff1), "w_ff2": f(w_ff2), "vecs": vecs, "psc": psc, "cmask": cm, "ident": ident,
        "r8": r8, "mcol": mcol, "invc": invc,
    }
    xpr, xsm, sg, spl = f(x_prompt), f(x_sample), f(state_gla), f(state_pool)
    in_maps = []
    for cidx in range(NCORES):
        m = dict(shared)
        m["xp"] = xpr[cidx]
        m["xs"] = np.ascontiguousarray(xsm[cidx * NSEQ_S:(cidx + 1) * NSEQ_S].reshape(128, D))
        m["sgla"] = np.ascontiguousarray(sg[:, cidx * NSEQ_S:(cidx + 1) * NSEQ_S])
        m["spool"] = np.ascontiguousarray(spl[:, cidx * NSEQ_S:(cidx + 1) * NSEQ_S])
        in_maps.append(m)
    res = run_bass_kernel_spmd(nc, in_maps, core_ids=list(range(NCORES)))
    R = res.results
    y_prompt = np.stack([R[i]["yp"] for i in range(NCORES)], axis=0)
    y_sample = np.concatenate([R[i]["ys"].reshape(NSEQ_S, LS, D) for i in range(NCORES)], axis=0)
    gla_p = np.stack([R[i]["gla_p"] for i in range(NCORES)], axis=1)
    gla_s = np.concatenate([R[i]["gla_s"] for i in range(NCORES)], axis=1)
    pool_p = np.stack([R[i]["pool_p"] for i in range(NCORES)], axis=1)
    pool_s = np.concatenate([R[i]["pool_s"] for i in range(NCORES)], axis=1)
    sgv_s = np.concatenate([R[i]["sgv_s"].reshape(DEPTH, NSEQ_S, LS, 512) for i in range(NCORES)], axis=1)
    return (y_prompt.astype(np.float32), y_sample.astype(np.float32), gla_p.astype(np.float32), gla_s.astype(np.float32),
            pool_p.astype(np.float32), pool_s.astype(np.float32), sgv_s.astype(np.float32))
```
